# Optimizing a Trainium2 kernel written in Bass

```python
import jax
import jax.numpy as jnp
from jax import lax
import numpy as np

D_MODEL = 1024
BATCH = 4
SEQ = 8192
DEPTH = 2

GRID_W = 64
CTX_LEN = 256
EPS = 1e-6

CONV_DIM = 512
CONV_WIDTH = 31
MLSTM_HEADS = 4
MLSTM_QK = 64
MLSTM_V = 128
MLSTM_CHUNK = 64
AB_SIZES = (CONV_DIM, CONV_DIM, MLSTM_HEADS * MLSTM_QK, MLSTM_HEADS * MLSTM_QK,
            MLSTM_HEADS * MLSTM_V, MLSTM_HEADS * MLSTM_V, 4 * MLSTM_HEADS)
AB_IN = sum(AB_SIZES)
AB_OUT = CONV_DIM + MLSTM_HEADS * MLSTM_V

MLA_HEADS = 16
MLA_Q_LORA = 512
MLA_KV_LORA = 256
MLA_NOPE = 64
MLA_ROPE = 32
MLA_V = 64
MLA_QK = MLA_NOPE + MLA_ROPE
MLA_IN = MLA_Q_LORA + MLA_KV_LORA + MLA_ROPE
ROPE_BASE = 10000.0
ATTN_BLOCK = 128

FFN_DIM = 2816
FFN_CONV_WIDTH = 3

kernel_name = 'hybrid_conformer_mlstm_mla_prefix_dit'


def _split_cols(a, sizes):
    return jnp.split(a, np.cumsum(sizes)[:-1].tolist(), axis=-1)


def _rms(x):
    xf = x.astype(jnp.float32)
    return (xf * lax.rsqrt(jnp.mean(xf * xf, axis=-1, keepdims=True) + EPS)).astype(x.dtype)


def _layer_norm(x, g, b):
    xf = x.astype(jnp.float32)
    mu = jnp.mean(xf, axis=-1, keepdims=True)
    var = jnp.mean(jnp.square(xf - mu), axis=-1, keepdims=True)
    return ((xf - mu) * lax.rsqrt(var + EPS)).astype(x.dtype) * g + b


def _modulate(x, shift, scale):
    return _rms(x) * (1 + scale) + shift


def _dwconv(x, w, b):
    k, ch = w.shape
    pad = (k - 1) // 2
    y = lax.conv_general_dilated(x, w[:, None, :].astype(x.dtype), (1,), [(pad, k - 1 - pad)],
                                 dimension_numbers=('NWC', 'WIO', 'NWC'), feature_group_count=ch)
    return y + b


def _zero_state(batch):
    return (jnp.zeros((batch, MLSTM_HEADS, MLSTM_V, MLSTM_QK), jnp.float32),
            jnp.zeros((batch, MLSTM_HEADS, MLSTM_QK), jnp.float32),
            jnp.zeros((batch, MLSTM_HEADS), jnp.float32))


def _mlstm_chunkwise(q, k, v, log_i, log_f, state):
    bsz, nh, t, _ = q.shape
    nc = t // MLSTM_CHUNK

    def chunks(a):
        return jnp.moveaxis(a.reshape(bsz, nh, nc, MLSTM_CHUNK, *a.shape[3:]), 2, 0)

    lower = jnp.tril(jnp.ones((MLSTM_CHUNK, MLSTM_CHUNK), bool))

    def step(carry, inp):
        c_mat, n_vec, m = carry
        qj, kj, vj, li, lf = inp
        b = jnp.cumsum(lf, axis=-1)
        log_d = jnp.where(lower, b[..., :, None] - b[..., None, :] + li[..., None, :], -jnp.inf)
        log_inter = b + m[..., None]
        m_row = jnp.maximum(log_inter, jnp.max(log_d, axis=-1))
        s = jnp.einsum('bhid,bhjd->bhij', qj, kj) * jnp.exp(log_d - m_row[..., None])
        inter = jnp.exp(log_inter - m_row)
        num = jnp.einsum('bhij,bhjv->bhiv', s, vj) + inter[..., None] * jnp.einsum('bhvd,bhid->bhiv', c_mat, qj)
        den = jnp.sum(s, axis=-1) + inter * jnp.einsum('bhd,bhid->bhi', n_vec, qj)
        h = num / jnp.maximum(jnp.abs(den), jnp.exp(-m_row))[..., None]
        b_last = b[..., -1]
        log_w = b_last[..., None] - b + li
        m_new = jnp.maximum(b_last + m, jnp.max(log_w, axis=-1))
        w = jnp.exp(log_w - m_new[..., None])
        decay = jnp.exp(b_last + m - m_new)
        c_new = decay[..., None, None] * c_mat + jnp.einsum('bhj,bhjv,bhjd->bhvd', w, vj, kj)
        n_new = decay[..., None] * n_vec + jnp.einsum('bhj,bhjd->bhd', w, kj)
        return (c_new, n_new, m_new), h

    state, h = lax.scan(step, state, (chunks(q), chunks(k), chunks(v), chunks(log_i), chunks(log_f)))
    return jnp.moveaxis(h, 0, 2).reshape(bsz, nh, t, -1), state


def _flip(t):
    return jnp.flip(t, axis=2)


def _ab_project(h, w_in, gate_bias):
    a, g, q, k, v, o, gates = _split_cols(h @ w_in, AB_SIZES)

    def heads(t, d):
        return t.reshape(*t.shape[:2], -1, d).transpose(0, 2, 1, 3).astype(jnp.float32)

    q = heads(q, MLSTM_QK)
    k = heads(k, MLSTM_QK) * (MLSTM_QK ** -0.5)
    v = heads(v, MLSTM_V)
    gates = (gates + gate_bias).astype(jnp.float32).transpose(0, 2, 1)
    ig_f, fg_f, ig_b, fg_b = jnp.split(gates, 4, axis=1)
    fwd = (ig_f, jax.nn.log_sigmoid(fg_f))
    bwd = (ig_b, jax.nn.log_sigmoid(fg_b))
    return a, g, q, k, v, o, fwd, bwd


def _conformer_mlstm_mixer(h_lat, h_ctx, w_in, gate_bias, conv_w, conv_b, ln_g, ln_b,
                           head_gain, w_out, with_ctx_out):
    lat = _ab_project(h_lat, w_in, gate_bias)
    ctx = _ab_project(h_ctx, w_in, gate_bias)
    zero = _zero_state(h_lat.shape[0])

    def scans(p, init_f, init_b):
        _, _, q, k, v, _, fwd, bwd = p
        h_f, st_f = _mlstm_chunkwise(q, k, v, fwd[0], fwd[1], init_f)
        h_b, st_b = _mlstm_chunkwise(_flip(q), _flip(k), _flip(v), _flip(bwd[0]), _flip(bwd[1]), init_b)
        return h_f + _flip(h_b), st_f, st_b

    h_ctx_seq, st_f, st_b = scans(ctx, zero, zero)
    h_lat_seq, _, _ = scans(lat, st_f, st_b)

    def merge(p, h_seq):
        a, g, _, _, _, o, _, _ = p
        u = _dwconv(a * jax.nn.sigmoid(g), conv_w, conv_b)
        u = jax.nn.silu(_layer_norm(u, ln_g, ln_b))
        hm = _rms(h_seq).transpose(0, 2, 1, 3)
        hm = hm.reshape(*hm.shape[:2], -1).astype(o.dtype) * head_gain * jax.nn.sigmoid(o)
        return jnp.concatenate([u, hm], axis=-1) @ w_out

    y_lat = merge(lat, h_lat_seq)
    y_ctx = merge(ctx, h_ctx_seq) if with_ctx_out else None
    return y_lat, y_ctx


def _axial_rope_tables(rows):
    row = jnp.repeat(jnp.arange(rows, dtype=jnp.float32), GRID_W)
    col = jnp.tile(jnp.arange(GRID_W, dtype=jnp.float32), rows)
    half = MLA_ROPE // 2
    inv_freq = ROPE_BASE ** (-jnp.arange(0, half, 2, dtype=jnp.float32) / half)
    ang = jnp.concatenate([row[:, None] * inv_freq, col[:, None] * inv_freq], axis=-1)
    return jnp.cos(ang), jnp.sin(ang)


def _rope_tail(t, cos, sin):
    nope, r = t[..., :MLA_NOPE], t[..., MLA_NOPE:]
    r = r.reshape(*r.shape[:-1], -1, 2)
    c = cos[None, :, None, :].astype(t.dtype)
    s = sin[None, :, None, :].astype(t.dtype)
    r1, r2 = r[..., 0], r[..., 1]
    rot = jnp.stack([r1 * c - r2 * s, r1 * s + r2 * c], axis=-1).reshape(*nope.shape[:-1], MLA_ROPE)
    return jnp.concatenate([nope, rot], axis=-1)


def _block_attention(q, k, v):
    bsz, tq, nh, dk = q.shape
    nb = tq // ATTN_BLOCK
    qb = jnp.moveaxis(q.reshape(bsz, nb, ATTN_BLOCK, nh, dk), 1, 0)
    scale = dk ** -0.5

    def one_block(qi):
        s = jnp.einsum('bqhd,bkhd->bhqk', qi, k).astype(jnp.float32) * scale
        p = jax.nn.softmax(s, axis=-1).astype(v.dtype)
        return jnp.einsum('bhqk,bkhd->bqhd', p, v)

    out = lax.map(one_block, qb)
    return jnp.moveaxis(out, 0, 1).reshape(bsz, tq, nh * v.shape[-1])


def _mla_mixer(h_lat, h_ctx, cos, sin, w_in, q_norm, kv_norm, w_uq, w_ukv, q_gain, k_gain,
               w_out, with_ctx_out):
    def queries(cq, rotate):
        q = (_rms(cq) * q_norm) @ w_uq
        q = _rms(q.reshape(*q.shape[:2], MLA_HEADS, MLA_QK)) * q_gain
        return _rope_tail(q, cos, sin) if rotate else q

    def keys_values(ckv, k_rope, rotate):
        ukv = ((_rms(ckv) * kv_norm) @ w_ukv).reshape(*ckv.shape[:2], MLA_HEADS, MLA_NOPE + MLA_V)
        k_nope, v = ukv[..., :MLA_NOPE], ukv[..., MLA_NOPE:]
        k_rope = jnp.broadcast_to(k_rope[:, :, None, :], (*k_nope.shape[:3], MLA_ROPE))
        k = _rms(jnp.concatenate([k_nope, k_rope], axis=-1)) * k_gain
        return (_rope_tail(k, cos, sin) if rotate else k), v

    cq_l, ckv_l, kr_l = _split_cols(h_lat @ w_in, (MLA_Q_LORA, MLA_KV_LORA, MLA_ROPE))
    ckv_c, kr_c = _split_cols(h_ctx @ w_in[:, MLA_Q_LORA:], (MLA_KV_LORA, MLA_ROPE))
    k_l, v_l = keys_values(ckv_l, kr_l, True)
    k_c, v_c = keys_values(ckv_c, kr_c, False)
    k_all = jnp.concatenate([k_c, k_l], axis=1)
    v_all = jnp.concatenate([v_c, v_l], axis=1)
    y_lat = _block_attention(queries(cq_l, True), k_all, v_all) @ w_out
    y_ctx = None
    if with_ctx_out:
        q_c = queries(h_ctx @ w_in[:, :MLA_Q_LORA], False)
        y_ctx = _block_attention(q_c, k_c, v_c) @ w_out
    return y_lat, y_ctx


def _conv_ffn(h, w_in, conv_w, conv_b, w_out):
    g, val = jnp.split(h @ w_in, 2, axis=-1)
    return (jax.nn.gelu(_dwconv(g, conv_w, conv_b), approximate=True) * val) @ w_out


def setup_inputs(seed: int = 0) -> dict:
    key = jax.random.key(seed)
    ks = iter(jax.random.split(key, 32))
    n_even = (DEPTH + 1) // 2
    n_odd = DEPTH // 2
    f32 = jnp.float32

    def nrm(shape, scale):
        return jax.random.normal(next(ks), shape, f32) * scale

    def gain(shape):
        return 1.0 + nrm(shape, 0.1)

    inp = {}
    inp['x'] = nrm((BATCH, SEQ, D_MODEL), 1.0)
    inp['c'] = nrm((BATCH, D_MODEL), 1.0)
    inp['ctx'] = nrm((BATCH, CTX_LEN, D_MODEL), 1.0)
    inp['c_ctx'] = nrm((D_MODEL,), 1.0)
    inp['ada_w'] = nrm((DEPTH, D_MODEL, 6 * D_MODEL), 0.5 * D_MODEL ** -0.5)
    inp['ada_b'] = nrm((DEPTH, 6 * D_MODEL), 0.02)
    inp['ab_w_in'] = nrm((n_even, D_MODEL, AB_IN), D_MODEL ** -0.5)
    ig = nrm((n_even, 2, 1, MLSTM_HEADS), 0.1)
    fg = jnp.linspace(3.0, 6.0, MLSTM_HEADS, dtype=f32) + nrm((n_even, 2, 1, MLSTM_HEADS), 0.1)
    inp['ab_gate_bias'] = jnp.concatenate([ig, fg], axis=2).reshape(n_even, 4 * MLSTM_HEADS)
    inp['ab_conv_w'] = nrm((n_even, CONV_WIDTH, CONV_DIM), CONV_WIDTH ** -0.5)
    inp['ab_conv_b'] = nrm((n_even, CONV_DIM), 0.02)
    inp['ab_ln_g'] = gain((n_even, CONV_DIM))
    inp['ab_ln_b'] = nrm((n_even, CONV_DIM), 0.02)
    inp['ab_head_gain'] = gain((n_even, MLSTM_HEADS * MLSTM_V))
    inp['ab_w_out'] = nrm((n_even, AB_OUT, D_MODEL), AB_OUT ** -0.5)
    inp['mla_w_in'] = nrm((n_odd, D_MODEL, MLA_IN), D_MODEL ** -0.5)
    inp['mla_q_norm'] = gain((n_odd, MLA_Q_LORA))
    inp['mla_kv_norm'] = gain((n_odd, MLA_KV_LORA))
    inp['mla_w_uq'] = nrm((n_odd, MLA_Q_LORA, MLA_HEADS * MLA_QK), MLA_Q_LORA ** -0.5)
    inp['mla_w_ukv'] = nrm((n_odd, MLA_KV_LORA, MLA_HEADS * (MLA_NOPE + MLA_V)), MLA_KV_LORA ** -0.5)
    inp['mla_q_gain'] = gain((n_odd, MLA_QK))
    inp['mla_k_gain'] = gain((n_odd, MLA_QK))
    inp['mla_w_out'] = nrm((n_odd, MLA_HEADS * MLA_V, D_MODEL), (MLA_HEADS * MLA_V) ** -0.5)
    inp['ffn_w_in'] = nrm((DEPTH, D_MODEL, 2 * FFN_DIM), D_MODEL ** -0.5)
    inp['ffn_conv_w'] = nrm((DEPTH, FFN_CONV_WIDTH, FFN_DIM), FFN_CONV_WIDTH ** -0.5)
    inp['ffn_conv_b'] = nrm((DEPTH, FFN_DIM), 0.02)
    inp['ffn_w_out'] = nrm((DEPTH, FFN_DIM, D_MODEL), FFN_DIM ** -0.5)
    return inp


def reference(x, c, ctx, c_ctx, ada_w, ada_b, ab_w_in, ab_gate_bias, ab_conv_w, ab_conv_b,
              ab_ln_g, ab_ln_b, ab_head_gain, ab_w_out, mla_w_in, mla_q_norm, mla_kv_norm,
              mla_w_uq, mla_w_ukv, mla_q_gain, mla_k_gain, mla_w_out, ffn_w_in, ffn_conv_w,
              ffn_conv_b, ffn_w_out):
    rows = x.shape[1] // GRID_W
    cos, sin = _axial_rope_tables(rows)
    xl, xc = x, ctx
    for layer in range(DEPTH):
        last = layer == DEPTH - 1
        j = layer // 2
        mod_l = jnp.split((jax.nn.silu(c) @ ada_w[layer] + ada_b[layer])[:, None, :], 6, axis=-1)
        mod_c = jnp.split(jax.nn.silu(c_ctx) @ ada_w[layer] + ada_b[layer], 6, axis=-1)
        h_lat = _modulate(xl, mod_l[0], mod_l[1])
        h_ctx = _modulate(xc, mod_c[0], mod_c[1])
        if layer % 2 == 0:
            y_lat, y_ctx = _conformer_mlstm_mixer(
                h_lat, h_ctx, ab_w_in[j], ab_gate_bias[j], ab_conv_w[j], ab_conv_b[j],
                ab_ln_g[j], ab_ln_b[j], ab_head_gain[j], ab_w_out[j], not last)
        else:
            y_lat, y_ctx = _mla_mixer(
                h_lat, h_ctx, cos, sin, mla_w_in[j], mla_q_norm[j], mla_kv_norm[j], mla_w_uq[j],
                mla_w_ukv[j], mla_q_gain[j], mla_k_gain[j], mla_w_out[j], not last)
        xl = xl + mod_l[2] * y_lat
        xl = xl + mod_l[5] * _conv_ffn(_modulate(xl, mod_l[3], mod_l[4]), ffn_w_in[layer],
                                       ffn_conv_w[layer], ffn_conv_b[layer], ffn_w_out[layer])
        if not last:
            xc = xc + mod_c[2] * y_ctx
            xc = xc + mod_c[5] * _conv_ffn(_modulate(xc, mod_c[3], mod_c[4]), ffn_w_in[layer],
                                           ffn_conv_w[layer], ffn_conv_b[layer], ffn_w_out[layer])
    return xl
```

```python
import numpy as np
from concourse.bass_utils import run_bass_kernel_spmd
from contextlib import ExitStack
import concourse.bass as bass
import concourse.mybir as mybir

F32 = mybir.dt.float32
BF16 = mybir.dt.bfloat16
AF = mybir.ActivationFunctionType
ALU = mybir.AluOpType


def _strides(shape):
    st = [1] * len(shape)
    for i in range(len(shape) - 2, -1, -1):
        st[i] = st[i + 1] * shape[i + 1]
    return st


class Sched:
    NDMA = 48

    def __init__(self, nc, es):
        self.nc = nc
        self.E = {'pe': nc.tensor, 'act': nc.scalar, 'dve': nc.vector, 'pool': nc.gpsimd, 'sp': nc.sync}
        self.sem = {k: es.enter_context(nc.semaphore('s_' + k)) for k in self.E}
        self.cnt = {k: 0 for k in self.E}
        self.seen = {k: {} for k in self.E}
        self.dsem = [es.enter_context(nc.semaphore('d%d' % i)) for i in range(self.NDMA)]
        self.dval = [0] * self.NDMA
        self.dn = 0
        self.dn_sw = 0
        self.acc = {}
        self.shapes = {}
        self.psum_names = set()
        self.nins = 0

    def box(self, ap):
        t = ap.tensor
        name = t.name
        shape = list(t.shape)
        st = _strides(shape)
        off = ap.offset
        lo = []
        for k in range(len(shape)):
            lo.append(off // st[k])
            off = off % st[k]
        hi = list(lo)
        for (step, count) in ap.ap:
            if count <= 1 or step == 0:
                continue
            step = abs(step)
            kk = None
            for k in range(len(shape)):
                if st[k] <= step and step % st[k] == 0:
                    kk = k
                    break
            if kk is None:
                return name, None
            hi[kk] += (step // st[kk]) * (count - 1)
            if hi[kk] >= shape[kk]:
                return name, None
        if name in self.psum_names:
            return name, ((lo[-1] // 512, hi[-1] // 512),)
        return name, tuple(zip(lo, hi))

    @staticmethod
    def _ov(a, b):
        if a is None or b is None:
            return True
        for (l1, h1), (l2, h2) in zip(a, b):
            if h1 < l2 or h2 < l1:
                return False
        return True

    @staticmethod
    def _inside(a, b):
        if b is None:
            return True
        if a is None:
            return False
        for (l1, h1), (l2, h2) in zip(a, b):
            if l1 < l2 or h1 > h2:
                return False
        return True

    def _wait(self, e, tok):
        kind = tok[0]
        if kind == 'dma':
            key = ('dma', tok[1])
            val = tok[2]
            sem = self.dsem[tok[1]]
        else:
            if kind == 'pe' and e == 'pe':
                return
            key = kind
            val = tok[1]
            sem = self.sem[kind]
        if self.seen[e].get(key, 0) >= val:
            return
        self.E[e].wait_ge(sem, val)
        self.nins += 1
        self.seen[e][key] = val

    def _deps(self, e, reads, writes):
        toks = set()
        rb = [self.box(a) for a in reads]
        wb = [self.box(a) for a in writes]
        for name, bx in rb:
            isp = name in self.psum_names
            for (b2, tok, isw, e2) in self.acc.get(name, ()):
                if (isw or (isp and e2 != e)) and self._ov(bx, b2):
                    toks.add(tok)
        for name, bx in wb:
            for (b2, tok, isw, e2) in self.acc.get(name, ()):
                if self._ov(bx, b2):
                    toks.add(tok)
        for tok in sorted(toks, key=str):
            self._wait(e, tok)
        return rb, wb

    def _record(self, e, tok, rb, wb):
        for name, bx in wb:
            lst = self.acc.setdefault(name, [])
            lst[:] = [x for x in lst if not self._inside(x[0], bx)]
            lst.append((bx, tok, True, e))
        for name, bx in rb:
            lst = self.acc.setdefault(name, [])
            if tok[0] != 'dma':
                lst[:] = [x for x in lst if not ((not x[2]) and x[3] == e and x[0] == bx and x[1][0] != 'dma')]
            lst.append((bx, tok, False, e))

    def op(self, e, fn, reads, writes, inc=True):
        rb, wb = self._deps(e, reads, writes)
        ins = fn(self.E[e])
        self.nins += 1
        if inc:
            self.cnt[e] += 1
            ins.then_inc(self.sem[e], 1)
            tok = (e, self.cnt[e])
        else:
            tok = (e, self.cnt[e] + 1)
        self._record(e, tok, rb, wb)
        return ins

    def dma(self, out, in_, q='sp'):
        rb, wb = self._deps(q, [in_], [out])
        half = self.NDMA // 2
        if q == 'pool':
            k = half + self.dn_sw % half
            self.dn_sw += 1
        else:
            k = self.dn % half
            self.dn += 1
        if self.dval[k] > 0:
            self._wait(q, ('dma', k, self.dval[k]))
        ins = self.E[q].dma_start(out=out, in_=in_)
        self.nins += 1
        self.dval[k] += 16
        ins.then_inc(self.dsem[k], 16)
        tok = ('dma', k, self.dval[k])
        self._record(q, tok, rb, wb)

    def barrier(self):
        for e in self.E:
            for e2 in self.E:
                if e2 != e and self.cnt[e2] > 0:
                    self._wait(e, (e2, self.cnt[e2]))
            if e != 'pe' and self.cnt[e] > 0:
                self._wait(e, (e, self.cnt[e]))
            for k in range(self.NDMA):
                if self.dval[k] > 0:
                    self._wait(e, ('dma', k, self.dval[k]))
        self.acc = {}

    def mm(self, out, lhsT, rhs, start=True, stop=True):
        return self.op('pe', lambda E: E.matmul(out, lhsT=lhsT, rhs=rhs, start=start, stop=stop),
                       [lhsT, rhs], [out], inc=stop)

    def mmg(self, out, pairs):
        n = len(pairs)
        for i, (l, r) in enumerate(pairs):
            self.mm(out, l, r, start=(i == 0), stop=(i == n - 1))

    def act(self, out, in_, func, bias=None, scale=None, accum_out=None, e='act'):
        kw = {}
        reads = [in_]
        writes = [out]
        if bias is not None:
            kw['bias'] = bias
            if not isinstance(bias, (int, float)):
                reads.append(bias)
        if scale is not None:
            kw['scale'] = scale
            if not isinstance(scale, (int, float)):
                reads.append(scale)
        if accum_out is not None:
            kw['accum_out'] = accum_out
            writes.append(accum_out)
        return self.op(e, lambda E: E.activation(out=out, in_=in_, func=func, **kw), reads, writes)

    def tt(self, out, in0, in1, op, e='dve'):
        return self.op(e, lambda E: E.tensor_tensor(out=out, in0=in0, in1=in1, op=op), [in0, in1], [out])

    def ts(self, out, in0, s1, s2, op0, op1=None, e='dve'):
        reads = [in0]
        for s in (s1, s2):
            if s is not None and not isinstance(s, (int, float)):
                reads.append(s)
        kw = {}
        if op1 is not None:
            kw['op1'] = op1
        return self.op(e, lambda E: E.tensor_scalar(out=out, in0=in0, scalar1=s1, scalar2=s2, op0=op0, **kw),
                       reads, [out])

    def stt(self, out, in0, scalar, in1, op0, op1, e='dve'):
        reads = [in0, in1]
        if not isinstance(scalar, (int, float)):
            reads.append(scalar)
        return self.op(e, lambda E: E.scalar_tensor_tensor(out=out, in0=in0, scalar=scalar, in1=in1, op0=op0, op1=op1),
                       reads, [out])

    def copy(self, out, in_, e='dve'):
        if e == 'act':
            return self.op(e, lambda E: E.activation(out=out, in_=in_, func=AF.Copy), [in_], [out])
        return self.op(e, lambda E: E.tensor_copy(out=out, in_=in_), [in_], [out])

    def memset(self, ap, val, e='pool'):
        return self.op(e, lambda E: E.memset(ap, val), [], [ap])

    def recip(self, out, in_):
        return self.op('dve', lambda E: E.reciprocal(out=out, in_=in_), [in_], [out])


class Phase:
    _n = 0

    def __init__(self, S):
        self.S = S
        self.es = ExitStack()
        Phase._n += 1
        self.pfx = 'p%d_' % Phase._n

    def __enter__(self):
        self.es.__enter__()
        return self

    def sb(self, name, shape, dt):
        return self.es.enter_context(self.S.nc.sbuf_tensor(self.pfx + name, list(shape), dt))

    def __exit__(self, *a):
        self.S.barrier()
        return self.es.__exit__(*a)
TT = 256
EPS = 1e-6
D = 1024


class G:
    pass


def run_zip(gens):
    gens = list(gens)
    while gens:
        nxt = []
        for ge in gens:
            try:
                next(ge)
                nxt.append(ge)
            except StopIteration:
                pass
        gens = nxt


def build(T, TC=256, dbg=(), upto='all'):
    nc = bass.Bass("TRN2", target_bir_lowering=False)
    es = ExitStack()
    with es:
        S = Sched(nc, es)
        g = G()
        g.nc, g.S, g.T, g.TC, g.TA = nc, S, T, TC, T + TC
        g.dbg = set(dbg)
        g.I = {}

        def inp(name, shape, dt=F32):
            g.I[name] = nc.dram_tensor(name, list(shape), dt, kind="ExternalInput").ap()
            return g.I[name]

        def scratch(name, shape, dt=F32):
            kind = "ExternalOutput" if name in g.dbg else "Internal"
            return nc.dram_tensor(name, list(shape), dt, kind=kind).ap()
        g.scratch = scratch
        TA = g.TA
        inp('xT', [D, T]); inp('ctxT', [D, TC]); inp('cc', [128, 8, 2])
        inp('ada_w', [2, D, 6 * D]); inp('ada_b', [128, 2, 48])
        inp('ab_w_in', [D, 2576]); inp('gate_bias', [128, 16]); inp('conv_w', [128, 4, 31])
        inp('ab_vec', [128, 4, 4])
        inp('ab_w_out', [D, D])
        inp('ffn_w_in', [2, D, 5632]); inp('ffn_cw', [128, 2, 22, 3]); inp('ffn_cb', [128, 2, 22])
        inp('ffn_w_out', [2, 2816, D])
        inp('mla_w_in', [D, 800]); inp('mla_qn', [128, 4]); inp('mla_kvn', [128, 2])
        inp('mla_w_uq', [512, 1536]); inp('mla_w_ukv', [256, 2048]); inp('mla_gain_rep', [128, 2, 96])
        inp('mla_w_out', [D, D])
        inp('ropeCS', [T, 2, 16])
        inp('c_ident', [128, 128]); inp('c_ones', [128, 128]); inp('c_triu', [128, 128]); inp('c_tril', [128, 128])
        g.outT = nc.dram_tensor('outT', [D, T // 2], F32, kind="ExternalOutput").ap()

        g.ps = []
        g.pd = []
        for i in range(4):
            t = es.enter_context(nc.psum_tensor('pd%d' % i, [128, 1024], F32))
            S.psum_names.add('pd%d' % i)
            g.pd.append(t)
            g.ps.append(t[:, 0:512]); g.ps.append(t[:, 512:1024])
        g.psi = 0

        def psum():
            t = g.ps[g.psi % 8]
            g.psi += 1
            return t
        g.psum = psum

        def gsb(name, shape, dt):
            return es.enter_context(nc.sbuf_tensor('g_' + name, list(shape), dt))
        g.ident = gsb('ident', [128, 128], F32); g.ones = gsb('ones', [128, 128], F32)
        g.identb = gsb('identb', [128, 128], BF16); g.onesb = gsb('onesb', [128, 128], BF16)
        g.mod = gsb('mod', [128, 2, 48, 2], F32)
        S.dma(g.ident[:], g.I['c_ident']); S.dma(g.ones[:], g.I['c_ones'])
        S.copy(g.identb[:], g.ident[:], e='pool'); S.copy(g.onesb[:], g.ones[:], e='pool')

        g.gluT = scratch('gluT', [512, TA], BF16)
        g.sigoT = scratch('sigoT', [512, TA], F32)
        g.qk32 = scratch('qk32', [TA, 512], F32)
        g.vtok = scratch('vtok', [TA, 512], BF16)
        g.gates = scratch('gates', [TA, 16], F32)
        g.hseq = [scratch('hseq%d' % d, [4, 128, TA], F32) for d in range(2)]
        g.x1T = scratch('x1T', [D, TA], F32)
        g.x2T = scratch('x2T', [D, TA], F32)
        g.x3T = scratch('x3T', [D, T // 2 + 2], F32)
        g.KT = scratch('KT', [16, 96, TA], BF16)
        g.Vtok = scratch('Vtok', [TA, 16, 128], BF16)
        g.QT = scratch('QT', [16, 96, T // 2 + TT], BF16)
        g.OT = scratch('OT', [D, T // 2 + 2], BF16)

        g.seqs = [dict(name='ctx', T=TC, o=0, r=1, xT=g.I['ctxT']), dict(name='lat', T=T, o=TC, r=0, xT=g.I['xT'])]

        phase_ada(g)
        if upto == 'ada':
            dump_mod(g)
        else:
            phase_l0a(g)
            if upto != 'l0a':
                phase_l0b(g)
            if upto not in ('l0a', 'l0b'):
                phase_l0c(g)
            if upto not in ('l0a', 'l0b', 'l0c'):
                TC = g.TC
                phase_ffn(g, 0, [dict(src=g.x1T, c0=0, c1=TC, v0=0, v1=TC, r=1, dst=g.x2T, d0=0),
                                 dict(src=g.x1T, c0=TC, c1=TA, v0=TC, v1=TA, r=0, dst=g.x2T, d0=TC)])
            if upto not in ('l0a', 'l0b', 'l0c', 'l0d'):
                phase_l1a(g)
            if upto not in ('l0a', 'l0b', 'l0c', 'l0d', 'l1a'):
                phase_l1b(g)
            if upto not in ('l0a', 'l0b', 'l0c', 'l0d', 'l1a', 'l1b'):
                phase_l1c(g)
            if upto not in ('l0a', 'l0b', 'l0c', 'l0d', 'l1a', 'l1b', 'l1c'):
                phase_ffn(g, 1, [dict(src=g.x3T, c0=0, c1=T // 2, v0=0, v1=T // 2 + 1, r=0, dst=g.outT, d0=0)])
        S.barrier()
    return nc


def dump_mod(g):
    S = g.S
    o = g.nc.dram_tensor('modout', [128, 2 * 48 * 2], F32, kind="ExternalOutput").ap()
    S.dma(o, g.mod[:].rearrange("p a b c -> p (a b c)"))


def load_w(g, ph, dst, src, K, N, stg, ranges=None, q='pool', ce=('pool', 'dve', 'act')):
    S = g.S
    v = src.rearrange("(k p) n -> p k n", p=128)
    SW = stg[0].shape[1]
    if ranges is None:
        ranges = [(0, N)]
    i = g.__dict__.setdefault('_lw', 0)
    for (r0, r1) in ranges:
        for n0 in range(r0, r1, SW):
            n1 = min(r1, n0 + SW)
            for k in range(K // 128):
                st = stg[i % len(stg)]
                i += 1
                S.dma(st[:, :n1 - n0], v[:, k, n0:n1], q=q)
                S.copy(dst[:, k, n0:n1], st[:, :n1 - n0], e=ce[i % len(ce)])
    g._lw = i


def phase_ada(g):
    S = g.S
    with Phase(S) as ph:
        cc = ph.sb('cc', [128, 8, 2], F32)
        sc = ph.sb('sc', [128, 8, 2], F32)
        adab = ph.sb('adab', [128, 2, 48], F32)
        wst = [ph.sb('adaw%d' % i, [128, 8, 512], F32) for i in range(2)]
        S.dma(cc[:], g.I['cc'])
        S.dma(adab[:], g.I['ada_b'])
        S.act(sc[:], cc[:], AF.Silu)
        for l in range(2):
            wv = g.I['ada_w'][l].rearrange("(k p) n -> p k n", p=128)
            for gi in range(12):
                w = wst[gi % 2]
                S.dma(w[:], wv[:, :, gi * 512:(gi + 1) * 512])
                for jj in range(4):
                    j = gi * 4 + jj
                    p = g.psum()
                    S.mmg(p[:, 0:2], [(w[:, k, jj * 128:(jj + 1) * 128], sc[:, k, :]) for k in range(8)])
                    S.ts(g.mod[:, l, j, :], p[:, 0:2], adab[:, l, j:j + 1], None, ALU.add)
            for m in (1, 4):
                S.ts(g.mod[:, l, m * 8:(m + 1) * 8, :], g.mod[:, l, m * 8:(m + 1) * 8, :], 1.0, None, ALU.add)


def rstd(S, out, in_, mul):
    S.ts(out, in_, mul, EPS, ALU.mult, ALU.add)
    S.recip(out, out)
    S.act(out, out, AF.Sqrt)


def modulate(g, xt, hT, W, l, m_shift, m_scale, r, sq, tn, rst):
    S = g.S
    S.act(sq[:, :, :W], xt, AF.Square)
    p = g.psum()
    ones = g.onesb if sq.dtype == BF16 else g.ones
    S.mmg(p[:, :W], [(ones[:], sq[:, k, :W]) for k in range(8)])
    rstd(S, rst[:, :W], p[:, :W], 1.0 / D)
    S.tt(tn[:, :, :W], xt, rst[:, :W].unsqueeze(1).to_broadcast([128, 8, W]), ALU.mult)
    for c in range(8):
        S.act(hT[:, c, :W], tn[:, c, :W], AF.Identity,
              bias=g.mod[:, l, m_shift * 8 + c, r:r + 1], scale=g.mod[:, l, m_scale * 8 + c, r:r + 1])


def modulate_g(g, xt, hT, W, l, m_shift, m_scale, r, sq, tn, rst):
    S = g.S
    S.act(sq[:, :, :W], xt, AF.Square)
    yield
    p = g.psum()
    S.mmg(p[:, :W], [(g.ones[:], sq[:, k, :W]) for k in range(8)])
    S.ts(rst[:, :W], p[:, :W], 1.0 / D, EPS, ALU.mult, ALU.add)
    S.recip(rst[:, :W], rst[:, :W])
    yield
    S.act(rst[:, :W], rst[:, :W], AF.Sqrt)
    yield
    S.tt(tn[:, :, :W], xt, rst[:, :W].unsqueeze(1).to_broadcast([128, 8, W]), ALU.mult)
    yield
    for c in range(8):
        S.act(hT[:, c, :W], tn[:, c, :W], AF.Identity,
              bias=g.mod[:, l, m_shift * 8 + c, r:r + 1], scale=g.mod[:, l, m_scale * 8 + c, r:r + 1])
    yield


def phase_l0a(g):
    S = g.S
    with Phase(S) as ph:
        Wab = ph.sb('Wab', [128, 8, 2576], BF16)
        stg = [ph.sb('stg%d' % i, [128, 2048], F32) for i in range(2)]
        load_w(g, ph, Wab, g.I['ab_w_in'], 1024, 2576, stg)
        gb = ph.sb('gb', [128, 16], F32)
        S.dma(gb[:], g.I['gate_bias'])
        xt = [ph.sb('xt%d' % i, [128, 8, TT], F32) for i in range(2)]
        sqb = [ph.sb('sq%d' % i, [128, 8, TT], F32) for i in range(2)]
        rstb = [ph.sb('rst%d' % i, [128, TT], F32) for i in range(2)]
        hT = [ph.sb('hT%d' % i, [128, 8, TT], BF16) for i in range(2)]
        sig = [ph.sb('sig%d' % i, [128, TT], F32) for i in range(4)]
        glu = [ph.sb('glu%d' % i, [128, 4, TT], BF16) for i in range(2)]
        so = [ph.sb('so%d' % i, [128, 4, TT], F32) for i in range(2)]
        qk32 = [ph.sb('qk32_%d' % i, [128, TT // 128, 512], F32) for i in range(2)]
        vt = [ph.sb('vt%d' % i, [128, TT // 128, 512], BF16) for i in range(2)]
        gt = [ph.sb('gt%d' % i, [128, TT // 128, 16], F32) for i in range(2)]
        tiles = []
        for seq in g.seqs:
            for ti in range(seq['T'] // TT):
                tiles.append((seq, ti))

        def tile(n):
            seq, ti = tiles[n]
            it = n
            xv = seq['xT'].rearrange("(k p) t -> p k t", p=128)
            t0 = ti * TT
            a0 = seq['o'] + t0
            x = xt[it % 2]; h = hT[it % 2]
            S.dma(x[:], xv[:, :, t0:t0 + TT])
            yield
            yield from modulate_g(g, x[:], h, TT, 0, 0, 1, seq['r'], sqb[it % 2], sqb[it % 2], rstb[it % 2])
            gl = glu[it % 2]; sg = so[it % 2]; qk = qk32[it % 2]; vv = vt[it % 2]; gg = gt[it % 2]
            for c in range(4):
                pa = g.psum(); pg = g.psum()
                S.mmg(pa[:, :TT], [(Wab[:, k, c * 128:(c + 1) * 128], h[:, k, :]) for k in range(8)])
                S.mmg(pg[:, :TT], [(Wab[:, k, 512 + c * 128:512 + (c + 1) * 128], h[:, k, :]) for k in range(8)])
                s_ = sig[2 * (it % 2) + c % 2]
                S.act(s_[:], pg[:, :TT], AF.Sigmoid)
                S.tt(gl[:, c, :], pa[:, :TT], s_[:], ALU.mult)
                yield
            S.dma(g.gluT.rearrange("(c p) t -> p c t", p=128)[:, :, a0:a0 + TT], gl[:], q='pool')
            for c in range(4):
                po = g.psum()
                S.mmg(po[:, :TT], [(Wab[:, k, 2048 + c * 128:2048 + (c + 1) * 128], h[:, k, :]) for k in range(8)])
                S.act(sg[:, c, :], po[:, :TT], AF.Sigmoid)
                if c % 2:
                    yield
            S.dma(g.sigoT.rearrange("(c p) t -> p c t", p=128)[:, :, a0:a0 + TT], sg[:], q='pool')
            for u in range(TT // 128):
                hs = slice(u * 128, (u + 1) * 128)
                pq = g.psum(); pv = g.psum(); pgt = g.psum()
                S.mmg(pq[:, 0:512], [(h[:, k, hs], Wab[:, k, 1024:1536]) for k in range(8)])
                S.mmg(pv[:, 0:512], [(h[:, k, hs], Wab[:, k, 1536:2048]) for k in range(8)])
                S.mmg(pgt[:, 0:16], [(h[:, k, hs], Wab[:, k, 2560:2576]) for k in range(8)])
                S.copy(qk[:, u, 0:256], pq[:, 0:256], e='act')
                S.act(qk[:, u, 256:512], pq[:, 256:512], AF.Copy, scale=0.125)
                S.copy(vv[:, u, :], pv[:, 0:512], e='dve')
                S.tt(gg[:, u, :], pgt[:, 0:16], gb[:], ALU.add)
                yield
            S.dma(g.qk32[a0:a0 + TT, :].rearrange("(u p) c -> p u c", p=128), qk[:], q='pool')
            S.dma(g.vtok[a0:a0 + TT, :].rearrange("(u p) c -> p u c", p=128), vv[:], q='pool')
            S.dma(g.gates[a0:a0 + TT, :].rearrange("(u p) c -> p u c", p=128), gg[:], q='pool')

        for n in range(0, len(tiles), 2):
            run_zip([tile(m) for m in range(n, min(n + 2, len(tiles)))])


def phase_l0b(g):
    S = g.S
    NCH = g.TA // 128
    NCC = g.TC // 128
    with Phase(S) as ph:
        triu = ph.sb('triu', [128, 128], F32); tril = ph.sb('tril', [128, 128], F32)
        S.dma(triu[:], g.I['c_triu']); S.dma(tril[:], g.I['c_tril'])
        U = [triu, tril]
        GT = ph.sb('GT', [128, NCH, 16], F32)
        S.dma(GT[:], g.gates.rearrange("(c p) g -> p c g", p=128))
        L1 = ph.sb('L1', [128, 2, NCH, 4], F32)
        Bc = ph.sb('Bc', [128, 2, NCH, 4], F32)
        Ecol = ph.sb('Ecol', [128, 2, NCH, 4], F32)
        Acol = ph.sb('Acol', [128, 2, NCH, 4], F32)
        Wcol = ph.sb('Wcol', [128, 2, NCH, 4], F32)
        DEC = ph.sb('DEC', [128, 2, NCH, 4], F32)
        DECP = ph.sb('DECP', [128, 2, NCH, 2], F32)
        for d in range(2):
            S.act(L1[:, d], GT[:, :, 4 + 8 * d:8 + 8 * d], AF.Exp, scale=-1.0)
            S.act(L1[:, d], L1[:, d], AF.Ln, bias=1.0)
            pb = g.psum()
            S.mmg(pb[:, 0:NCH * 4], [(U[d][:], L1[:, d].rearrange("p c h -> p (c h)"))])
            S.copy(Bc[:, d].rearrange("p c h -> p (c h)"), pb[:, 0:NCH * 4], e='dve')
            pt = g.psum()
            S.mmg(pt[:, 0:NCH * 4], [(g.ones[:], L1[:, d].rearrange("p c h -> p (c h)"))])
            S.act(DEC[:, d].rearrange("p c h -> p (c h)"), pt[:, 0:NCH * 4], AF.Exp, scale=-1.0)
            S.act(Ecol[:, d], Bc[:, d], AF.Exp, scale=-1.0)
            S.tt(Acol[:, d], Bc[:, d], GT[:, :, 8 * d:8 * d + 4], ALU.add)
            S.act(Acol[:, d], Acol[:, d], AF.Exp)
            S.tt(Wcol[:, d], Acol[:, d], DEC[:, d], ALU.mult)
            dv = DEC[:, d].rearrange("p c (q two) -> p c q two", two=2)
            S.copy(DECP[0:64, d], dv[0:64, :, :, 0], e='dve')
            S.copy(DECP[64:128, d], dv[64:128, :, :, 1], e='dve')
        NB = 2
        XQ = [[ph.sb('XQ%d_%d' % (d, i), [128, 512], F32) for i in range(NB)] for d in range(2)]
        X = [[ph.sb('X%d_%d' % (d, i), [128, 514], BF16) for i in range(NB)] for d in range(2)]
        qs = [[ph.sb('qs%d_%d' % (d, i), [128, 4, 128], BF16) for i in range(NB)] for d in range(2)]
        ka = [[ph.sb('ka%d_%d' % (d, i), [128, 4, 64], BF16) for i in range(NB)] for d in range(2)]
        kw = [[ph.sb('kw%d_%d' % (d, i), [128, 4, 64], BF16) for i in range(NB)] for d in range(2)]
        QKT = [[ph.sb('QKT%d_%d' % (d, i), [128, 4, 128], BF16) for i in range(NB)] for d in range(2)]
        KTs = [[ph.sb('KT%d_%d' % (d, i), [128, 2, 128], BF16) for i in range(NB)] for d in range(2)]
        PFs = [[ph.sb('PF%d_%d' % (d, i), [128, 4, 128], F32) for i in range(NB)] for d in range(2)]
        PTs = [[ph.sb('PT%d_%d' % (d, i), [128, 4, 128], BF16) for i in range(NB)] for d in range(2)]
        dn = [[ph.sb('dn%d_%d' % (d, i), [128, 4, 128], F32) for i in range(NB)] for d in range(2)]
        ho = [[ph.sb('ho%d_%d' % (d, i), [128, 4, 128], F32) for i in range(NB)] for d in range(2)]
        St = [[ph.sb('St%d_%d' % (d, p), [128, 257], F32) for p in range(2)] for d in range(2)]
        Sb = [[[ph.sb('Sb%d_%d_%d' % (d, p, i), [128, 256], BF16) for i in range(2)] for p in range(2)] for d in range(2)]
        nr = [[[ph.sb('nr%d_%d_%d' % (d, p, i), [128, 128], BF16) for i in range(2)] for p in range(2)] for d in range(2)]
        for d in range(2):
            for i in range(NB):
                S.memset(X[d][i][:, 256:257], 1.0); S.memset(X[d][i][:, 513:514], 1.0)
                S.memset(qs[d][i][:], 0.0)
            for p in range(2):
                S.memset(St[d][p][:], 0.0)
                S.memset(Sb[d][p][0][:], 0.0)
                S.memset(nr[d][p][0][:], 0.0)
        order = [list(range(NCH)), list(range(NCC - 1, -1, -1)) + list(range(NCH - 1, NCC - 1, -1))]

        def bcol(col, d, c):
            return col[:, d, c, :].unsqueeze(2).to_broadcast([128, 4, 64])

        def part1(s, d):
            c = order[d][s]
            b = s % NB
            x = X[d][b]; xq = XQ[d][b]
            S.dma(xq[:], g.qk32[c * 128:(c + 1) * 128, :])
            S.dma(x[:, 0:256], g.vtok[c * 128:(c + 1) * 128, 0:256])
            S.dma(x[:, 257:513], g.vtok[c * 128:(c + 1) * 128, 256:512])
            yield
            k3 = xq[:, 256:512].rearrange("p (h e) -> p h e", h=4)
            q5 = qs[d][b][:].rearrange("p (pp r) (rr e) -> p pp r rr e", r=2, rr=2)
            q4 = xq[:, 0:256].rearrange("p (pp r e) -> p pp r e", pp=2, r=2)
            for r in range(2):
                S.tt(q5[:, :, r, r, :], q4[:, :, r, :],
                     Ecol[:, d, c, :].rearrange("p (pp r) -> p pp r", r=2)[:, :, r].unsqueeze(2).to_broadcast([128, 2, 64]),
                     ALU.mult)
            S.tt(ka[d][b][:], k3, bcol(Acol, d, c), ALU.mult)
            S.tt(kw[d][b][:], k3, bcol(Wcol, d, c), ALU.mult, e='pool')
            yield
            ptq = g.psum(); ptk = g.psum()
            for h in range(4):
                S.mmg(ptq[:, h * 128:(h + 1) * 128], [(qs[d][b][:, h, :], g.identb[:])])
            for p in range(2):
                S.mmg(ptk[:, p * 128:(p + 1) * 128],
                      [(ka[d][b][:, 2 * p:2 * p + 2, :].rearrange("p h e -> p (h e)"), g.identb[:])])
            QT = QKT[d][b]; KT = KTs[d][b]
            S.copy(QT[:].rearrange("p a n -> p (a n)"), ptq[:, 0:512], e='act')
            S.copy(KT[:].rearrange("p a n -> p (a n)"), ptk[:, 0:256], e='act')
            yield
            pS = g.psum()
            for h in range(4):
                S.mmg(pS[:, h * 128:(h + 1) * 128], [(KT[:, h // 2, :], QT[:, h, :])])
            PT = PTs[d][b]; PF = PFs[d][b]
            S.tt(PF[:], pS[:, 0:512].rearrange("p (h i) -> p h i", h=4),
                 U[d][:].unsqueeze(1).to_broadcast([128, 4, 128]), ALU.mult)
            yield
            S.copy(PT[:], PF[:], e='act')

        def part2(s, d):
            c = order[d][s]
            b = s % NB
            sb = s % 2
            x = X[d][b]; QT = QKT[d][b]; PT = PTs[d][b]; PF = PFs[d][b]
            pN = g.psum(); pD = g.psum()
            for h in range(4):
                p, r = h // 2, h % 2
                base = p * 257
                S.mmg(pN[:, h * 128:(h + 1) * 128],
                      [(x[:, base + r * 128:base + (r + 1) * 128], PT[:, h, :]),
                       (Sb[d][p][sb][:, r * 128:(r + 1) * 128], QT[:, h, :])])
            S.mm(pD[:, 0:512], g.ones[:], PF[:].rearrange("p h i -> p (h i)"), start=True, stop=False)
            for h in range(4):
                S.mm(pD[:, h * 128:(h + 1) * 128], nr[d][h // 2][sb][:], QT[:, h, :], start=False, stop=(h == 3))
            dnn = dn[d][b]; hoo = ho[d][b]
            S.act(dnn[:].rearrange("p h i -> p (h i)"), pD[:, 0:512], AF.Abs)
            S.ts(dnn[:], dnn[:], 1.0, None, ALU.max)
            S.recip(dnn[:], dnn[:])
            S.tt(hoo[:].rearrange("p h i -> p (h i)"), pN[:, 0:512], dnn[:].rearrange("p h i -> p (h i)"), ALU.mult)
            yield
            S.dma(g.hseq[d][:, :, c * 128:(c + 1) * 128].rearrange("h v t -> v h t"), hoo[:], q='pool')
            for p in range(2):
                base = p * 257
                pU = g.psum()
                S.mmg(pU[:, 0:257], [(kw[d][b][:, 2 * p:2 * p + 2, :].rearrange("p h e -> p (h e)"), x[:, base:base + 257])])
                S.stt(St[d][p][:], St[d][p][:], DECP[:, d, c, p:p + 1], pU[:, 0:257], ALU.mult, ALU.add)
                yield
                S.copy(Sb[d][p][1 - sb][:], St[d][p][:, 0:256], e='act')
                S.copy(nr[d][p][1 - sb][:], St[d][p][:, 256:257].to_broadcast([128, 128]), e='act')

        run_zip([part1(0, d) for d in range(2)])
        for s in range(NCH):
            chains = [part2(s, d) for d in range(2)]
            if s + 1 < NCH:
                chains += [part1(s + 1, d) for d in range(2)]
            run_zip(chains)


def phase_l0c(g):
    S = g.S
    with Phase(S) as ph:
        stg = [ph.sb('stg%d' % i, [128, 2048], F32) for i in range(2)]
        Wout = ph.sb('Wout', [128, 8, 1024], BF16)
        load_w(g, ph, Wout, g.I['ab_w_out'], 1024, 1024, stg)
        cw = ph.sb('cw', [128, 4, 31], F32); vec = ph.sb('vec', [128, 4, 4], F32)
        S.dma(cw[:], g.I['conv_w']); S.dma(vec[:], g.I['ab_vec'])
        diagW = ph.sb('diagW', [128, 4, 31, 128], BF16)
        n = 0
        for c in range(4):
            for k in range(31):
                if n % 2:
                    S.act(diagW[:, c, k, :], g.identb[:], AF.Copy, scale=cw[:, c, k:k + 1])
                else:
                    S.ts(diagW[:, c, k, :], g.identb[:], cw[:, c, k:k + 1], None, ALU.mult)
                n += 1
        HW = TT + 30
        GH = [ph.sb('GH%d' % i, [128, 4, HW], BF16) for i in range(2)]
        upreb = [ph.sb('upre%d' % i, [128, 4, TT], F32) for i in range(2)]
        usqb = [ph.sb('usq%d' % i, [128, 4, TT], F32) for i in range(2)]
        meanb = [ph.sb('mean%d' % i, [128, TT], F32) for i in range(2)]
        msqb = [ph.sb('msq%d' % i, [128, TT], F32) for i in range(2)]
        rsb = [ph.sb('rs%d' % i, [128, TT], F32) for i in range(2)]
        hf = [ph.sb('hf%d' % i, [128, 4, TT], F32) for i in range(2)]
        hb = [ph.sb('hb%d' % i, [128, 4, TT], F32) for i in range(2)]
        sgo = [ph.sb('sgo%d' % i, [128, 4, TT], F32) for i in range(2)]
        hsqb = [ph.sb('hsq%d' % i, [128, 4, TT], F32) for i in range(2)]
        rshb = [ph.sb('rsh%d' % i, [128, 4, TT], F32) for i in range(2)]
        cat = [ph.sb('cat%d' % i, [128, 8, TT], BF16) for i in range(2)]
        xt = [ph.sb('xt%d' % i, [128, 8, TT], F32) for i in range(2)]
        gluv = g.gluT.rearrange("(c p) t -> p c t", p=128)
        sigv = g.sigoT.rearrange("(c p) t -> p c t", p=128)
        x1v = g.x1T.rearrange("(k p) t -> p k t", p=128)
        tiles = []
        for seq in g.seqs:
            for ti in range(seq['T'] // TT):
                tiles.append((seq, ti))

        def tile(n):
            seq, ti = tiles[n]
            xv = seq['xT'].rearrange("(k p) t -> p k t", p=128)
            o, Ts, r = seq['o'], seq['T'], seq['r']
            t0 = ti * TT
            a0 = o + t0
            b = n % 2
            upre, usq, mean, msq, rs, hsq, rsh = upreb[b], usqb[b], meanb[b], msqb[b], rsb[b], hsqb[b], rshb[b]
            gh = GH[b]
            lo = max(t0 - 15, 0); hi = min(t0 + TT + 15, Ts)
            if lo > t0 - 15:
                S.memset(gh[:, :, 0:lo - (t0 - 15)], 0.0)
            if hi < t0 + TT + 15:
                S.memset(gh[:, :, hi - (t0 - 15):HW], 0.0)
            S.dma(gh[:, :, lo - (t0 - 15):hi - (t0 - 15)], gluv[:, :, o + lo:o + hi])
            S.dma(hf[b][:], g.hseq[0][:, :, a0:a0 + TT].rearrange("h v t -> v h t"))
            S.dma(hb[b][:], g.hseq[1][:, :, a0:a0 + TT].rearrange("h v t -> v h t"))
            S.dma(sgo[b][:], sigv[:, :, a0:a0 + TT])
            S.dma(xt[b][:], xv[:, :, t0:t0 + TT])
            ct = cat[b]
            yield
            for c in range(4):
                pc = g.psum()
                S.mmg(pc[:, :TT], [(diagW[:, c, k, :], gh[:, c, k:k + TT]) for k in range(31)])
                S.act(upre[:, c, :], pc[:, :TT], AF.Identity, bias=vec[:, 0, c:c + 1])
                yield
            S.act(usq[:], upre[:], AF.Square)
            S.tt(hf[b][:], hf[b][:], hb[b][:], ALU.add, e='pool')
            yield
            S.act(hsq[:], hf[b][:], AF.Square)
            p1 = g.psum(); p2 = g.psum()
            S.mmg(p1[:, :TT], [(g.ones[:], upre[:, c, :]) for c in range(4)])
            S.mmg(p2[:, :TT], [(g.ones[:], usq[:, c, :]) for c in range(4)])
            S.ts(mean[:], p1[:, :TT], 1.0 / 512, None, ALU.mult)
            S.tt(msq[:], mean[:], mean[:], ALU.mult)
            S.stt(rs[:], p2[:, :TT], 1.0 / 512, msq[:], ALU.mult, ALU.subtract)
            yield
            S.ts(rs[:], rs[:], EPS, None, ALU.add)
            S.recip(rs[:], rs[:])
            yield
            S.act(rs[:], rs[:], AF.Sqrt)
            S.tt(upre[:], upre[:], mean[:].unsqueeze(1).to_broadcast([128, 4, TT]), ALU.subtract)
            yield
            S.tt(upre[:], upre[:], rs[:].unsqueeze(1).to_broadcast([128, 4, TT]), ALU.mult)
            yield
            for c in range(4):
                S.act(ct[:, c, :], upre[:, c, :], AF.Silu, bias=vec[:, 2, c:c + 1], scale=vec[:, 1, c:c + 1])
            for hh in range(2):
                pp = g.psum()
                for j in range(2):
                    S.mmg(pp[:, j * TT:(j + 1) * TT], [(g.ones[:], hsq[:, hh * 2 + j, :])])
                rv = rsh[:, hh * 2:hh * 2 + 2, :].rearrange("p h t -> p (h t)")
                S.ts(rv, pp[:, 0:2 * TT], 1.0 / 128, EPS, ALU.mult, ALU.add)
                S.recip(rv, rv)
                yield
                S.act(rv, rv, AF.Sqrt)
            yield
            S.tt(hf[b][:], hf[b][:], rsh[:], ALU.mult)
            yield
            for h in range(4):
                S.stt(ct[:, 4 + h, :], hf[b][:, h, :], vec[:, 3, h:h + 1], sgo[b][:, h, :], ALU.mult, ALU.mult)
            yield
            for oc in range(8):
                py = g.psum()
                S.mmg(py[:, :TT], [(Wout[:, k, oc * 128:(oc + 1) * 128], ct[:, k, :]) for k in range(8)])
                S.stt(xt[b][:, oc, :], py[:, :TT], g.mod[:, 0, 16 + oc, r:r + 1], xt[b][:, oc, :], ALU.mult, ALU.add)
                if oc % 2:
                    yield
            S.dma(x1v[:, :, a0:a0 + TT], xt[b][:], q='pool')

        for n in range(0, len(tiles), 2):
            run_zip([tile(m) for m in range(n, min(n + 2, len(tiles)))])


def phase_ffn(g, l, segs):
    S = g.S
    W = TT + 2
    with Phase(S) as ph:
        stg = [ph.sb('stg%d' % i, [128, 1024], F32) for i in range(2)]
        Win = ph.sb('Win', [128, 8, 5632], BF16)
        Wo = ph.sb('Wo', [128, 22, 1024], BF16)
        cw = ph.sb('fcw', [128, 2, 22, 3], F32); cb = ph.sb('fcb', [128, 2, 22], F32)
        S.dma(cw[:], g.I['ffn_cw']); S.dma(cb[:], g.I['ffn_cb'])
        xt = [ph.sb('xt%d' % i, [128, 8, W], F32) for i in range(2)]
        tmp = ph.sb('tmp', [128, 8, W], F32)
        sqh = ph.sb('sqh', [128, 8, W], BF16)
        rst = ph.sb('rst', [128, W], F32)
        hT = [ph.sb('hT%d' % i, [128, 8, W], BF16) for i in range(2)]
        act = [ph.sb('act%d' % i, [128, 22, TT], BF16) for i in range(2)]
        ta = [ph.sb('ta%d' % i, [128, TT], F32) for i in range(2)]
        tb = [ph.sb('tb%d' % i, [128, TT], F32) for i in range(2)]
        def load_x(sg, t0, x):
            sv = sg['src'].rearrange("(k p) t -> p k t", p=128)
            lo = max(t0 - 1, sg['v0']); hi = min(t0 + TT + 1, sg['v1'])
            if lo > t0 - 1:
                S.memset(x[:, :, 0:1], 0.0)
            if hi < t0 + TT + 1:
                S.memset(x[:, :, W - 1:W], 0.0)
            S.dma(x[:, :, lo - (t0 - 1):hi - (t0 - 1)], sv[:, :, lo:hi])
        load_x(segs[0], segs[0]['c0'], xt[0])
        wr = {0: [(0, 1024), (2816, 3840)], 8: [(1024, 2048), (3840, 4864)], 16: [(2048, 2816), (4864, 5632)]}
        it = 0
        for sg in segs:
            sv = sg['src'].rearrange("(k p) t -> p k t", p=128)
            dv = sg['dst'].rearrange("(k p) t -> p k t", p=128)
            r = sg['r']
            for t0 in range(sg['c0'], sg['c1'], TT):
                b = it % 2
                x = xt[b]; h = hT[b]; a = act[b]
                lo = max(t0 - 1, sg['v0']); hi = min(t0 + TT + 1, sg['v1'])
                if it > 0:
                    load_x(sg, t0, x)
                modulate(g, x[:], h, W, l, 3, 4, r, sqh, tmp, rst)
                if lo > t0 - 1:
                    S.memset(h[:, :, 0:1], 0.0)
                if hi < t0 + TT + 1:
                    S.memset(h[:, :, W - 1:W], 0.0)
                for ch in range(22):
                    if it == 0 and ch in wr:
                        load_w(g, ph, Win, g.I['ffn_w_in'][l], 1024, 5632, stg, ranges=wr[ch], q='sp', ce=('dve', 'act'))
                    pg = g.psum(); pv = g.psum()
                    S.mmg(pg[:, 0:W], [(Win[:, k, ch * 128:(ch + 1) * 128], h[:, k, 0:W]) for k in range(8)])
                    S.mmg(pv[:, 0:TT], [(Win[:, k, 2816 + ch * 128:2816 + (ch + 1) * 128], h[:, k, 1:TT + 1]) for k in range(8)])
                    t_a = ta[ch % 2]; t_b = tb[ch % 2]
                    S.act(t_a[:], pg[:, 1:TT + 1], AF.Identity, bias=cb[:, l, ch:ch + 1], scale=cw[:, l, ch, 1:2])
                    S.stt(t_a[:], pg[:, 0:TT], cw[:, l, ch, 0:1], t_a[:], ALU.mult, ALU.add)
                    S.stt(t_a[:], pg[:, 2:TT + 2], cw[:, l, ch, 2:3], t_a[:], ALU.mult, ALU.add)
                    S.act(t_b[:], t_a[:], AF.Gelu_apprx_tanh)
                    S.tt(a[:, ch, :], pv[:, 0:TT], t_b[:], ALU.mult)
                if it == 0:
                    load_w(g, ph, Wo, g.I['ffn_w_out'][l], 2816, 1024, stg, q='sp', ce=('dve', 'act'))
                for oc in range(8):
                    py = g.psum()
                    S.mmg(py[:, :TT], [(Wo[:, k, oc * 128:(oc + 1) * 128], a[:, k, :]) for k in range(22)])
                    S.stt(x[:, oc, 1:TT + 1], py[:, :TT], g.mod[:, l, 40 + oc, r:r + 1], x[:, oc, 1:TT + 1], ALU.mult, ALU.add)
                d = sg['d0'] + (t0 - sg['c0'])
                S.dma(dv[:, :, d:d + TT], x[:, :, 1:TT + 1], q='pool')
                it += 1


def phase_l1a(g):
    S = g.S
    T, TC, TA = g.T, g.TC, g.TA
    NU = TT // 128
    with Phase(S) as ph:
        tmpb = [ph.sb('tmp%d' % i, [128, 8, TT], F32) for i in range(2)]
        stg = [tmpb[i][:].rearrange("p k t -> p (k t)")[:, 0:1024] for i in range(2)]
        Win = ph.sb('Win', [128, 8, 800], BF16)
        load_w(g, ph, Win, g.I['mla_w_in'], 1024, 800, stg)
        Wuq = ph.sb('Wuq', [128, 4, 1536], BF16)
        load_w(g, ph, Wuq, g.I['mla_w_uq'], 512, 1536, stg)
        Wkv = ph.sb('Wkv', [128, 2, 2048], BF16)
        load_w(g, ph, Wkv, g.I['mla_w_ukv'], 256, 2048, stg)
        Wk = ph.sb('Wk', [128, 2, 16, 64], BF16)
        Wv = ph.sb('Wv', [128, 2, 16, 64], BF16)
        for k in range(2):
            w4 = Wkv[:, k, :].rearrange("p (h c) -> p h c", h=16)
            S.copy(Wk[:, k, :, :], w4[:, :, 0:64], e='pool')
            S.copy(Wv[:, k, :, :], w4[:, :, 64:128], e='pool')
        qn_ = ph.sb('qn_', [128, 4], F32); kvn = ph.sb('kvn', [128, 2], F32)
        gain = ph.sb('gain', [128, 2, 96], F32)
        S.dma(qn_[:], g.I['mla_qn']); S.dma(kvn[:], g.I['mla_kvn']); S.dma(gain[:], g.I['mla_gain_rep'])
        xt = [ph.sb('xt%d' % i, [128, 8, TT], F32) for i in range(1)] * 2
        rstb = [ph.sb('rst%d' % i, [128, TT], F32) for i in range(2)]
        hT = [ph.sb('hT%d' % i, [128, 8, TT], BF16) for i in range(2)]
        cqfb = [ph.sb('cqf%d' % i, [128, 4, TT], F32) for i in range(2)]
        cqnb = [ph.sb('cqn%d' % i, [128, 4, TT], BF16) for i in range(2)]
        ckfb = [tmpb[i][:, 6:8, :] for i in range(2)]
        cknb = [ph.sb('ckn%d' % i, [128, 2, TT], BF16) for i in range(2)]
        rs2b = [ph.sb('rs2_%d' % i, [128, TT], F32) for i in range(2)]
        cs = [ph.sb('cs%d' % i, [128, NU, 2, 16], F32) for i in range(2)]
        src = [ph.sb('src%d' % i, [128, 16, 96], F32) for i in range(4)]
        sqb = [ph.sb('sqb%d' % i, [128, 16, 96], F32) for i in range(4)]
        ssum = [ph.sb('ssum%d' % i, [128, 16], F32) for i in range(4)]
        krs = [ph.sb('krs%d' % i, [128, 32], F32) for i in range(4)]
        ra = [ph.sb('ra%d' % i, [128, 16, 16], F32) for i in range(4)]
        rb = [ph.sb('rb%d' % i, [128, 16, 16], F32) for i in range(4)]
        rc = [ph.sb('rc%d' % i, [128, 16, 16], F32) for i in range(4)]
        rd = [ph.sb('rd%d' % i, [128, 16, 16], F32) for i in range(4)]
        dstb = [ph.sb('dstb%d' % i, [128, 16, 128], BF16) for i in range(4)]
        for i in range(4):
            S.memset(dstb[i][:], 0.0)
        KTt = [ph.sb('KTt%d' % i, [128, 16, TT], BF16) for i in range(1)] * 2
        QTt = [ph.sb('QTt%d' % i, [128, 16, TT], BF16) for i in range(1)] * 2
        vsb = [ph.sb('vsb%d' % i, [128, 16, 128], BF16) for i in range(2)]
        for i in range(2):
            v4 = vsb[i][:].rearrange("p (hp two) c -> p hp two c", two=2)
            S.memset(v4[:, :, 0, 64:128], 1.0)
            S.memset(v4[:, :, 1, 0:64], 1.0)
        cnt = [0]

        def nr_tok(i, gi, rope, outT, u):
            x_ = src[i]; d_ = dstb[i]
            S.act(sqb[i][:], x_[:], AF.Square)
            yield
            S.op('dve', lambda E: E.reduce_sum(out=ssum[i][:], in_=sqb[i][:], axis=mybir.AxisListType.X), [sqb[i][:]], [ssum[i][:]])
            yield
            S.ts(ssum[i][:], ssum[i][:], 1.0 / 96, EPS, ALU.mult, ALU.add)
            S.recip(ssum[i][:], ssum[i][:])
            yield
            S.act(ssum[i][:], ssum[i][:], AF.Sqrt)
            yield
            S.tt(x_[:], x_[:], ssum[i][:].unsqueeze(2).to_broadcast([128, 16, 96]), ALU.mult)
            yield
            gt_ = gain[:, gi, :].unsqueeze(1).to_broadcast([128, 16, 96])
            if rope is None:
                S.tt(d_[:, :, 0:96], x_[:], gt_, ALU.mult, e='pool')
                yield
            else:
                S.tt(x_[:], x_[:], gt_, ALU.mult, e='pool')
                yield
                S.copy(d_[:, :, 0:64], x_[:, :, 0:64], e='act')
                r4 = x_[:, :, 64:96].rearrange("p h (q two) -> p h q two", two=2)
                o4 = d_[:, :, 64:96].rearrange("p h (q two) -> p h q two", two=2)
                c_ = rope[:, u, 0, :].unsqueeze(1).to_broadcast([128, 16, 16])
                s_ = rope[:, u, 1, :].unsqueeze(1).to_broadcast([128, 16, 16])
                S.tt(ra[i][:], r4[:, :, :, 0], c_, ALU.mult)
                S.tt(rb[i][:], r4[:, :, :, 1], s_, ALU.mult, e='pool')
                S.tt(rc[i][:], r4[:, :, :, 0], s_, ALU.mult)
                S.tt(rd[i][:], r4[:, :, :, 1], c_, ALU.mult, e='pool')
                yield
                S.tt(o4[:, :, :, 0], ra[i][:], rb[i][:], ALU.subtract)
                S.tt(o4[:, :, :, 1], rc[i][:], rd[i][:], ALU.add, e='pool')
                yield
            for hb in range(4):
                pT = g.psum()
                for j in range(4):
                    S.mmg(pT[:, j * 128:(j + 1) * 128], [(d_[:, hb * 4 + j, :], g.identb[:])])
                S.copy(outT[:, hb * 4:hb * 4 + 4, u * 128:(u + 1) * 128],
                       pT[:, 0:512].rearrange("p (j t) -> p j t", j=4), e='act')
                if hb % 2:
                    yield

        def k_chain(i, u, h, rope, kt_, a0, ckn):
            us = slice(u * 128, (u + 1) * 128)
            for half in range(2):
                pk = g.psum()
                S.mmg(pk[:, 0:512], [(ckn[:, k, us], Wk[:, k, half * 8:(half + 1) * 8, :].rearrange("p h c -> p (h c)"))
                                     for k in range(2)])
                S.copy(src[i][:, half * 8:(half + 1) * 8, 0:64], pk[:, 0:512].rearrange("p (h c) -> p h c", h=8),
                       e=('act' if half else 'dve'))
            pkr = g.psum()
            S.mmg(pkr[:, 0:32], [(h[:, k, us], Win[:, k, 768:800]) for k in range(8)])
            S.copy(krs[i][:], pkr[:, 0:32], e='act')
            yield
            S.copy(src[i][:, :, 64:96], krs[i][:].unsqueeze(1).to_broadcast([128, 16, 32]), e='pool')
            yield
            yield from nr_tok(i, 1, rope, kt_, u)
            vs = vsb[u % 2]
            for half in range(2):
                pv = g.psum()
                S.mmg(pv[:, 0:512], [(ckn[:, k, us],
                                      Wv[:, k, half * 8:(half + 1) * 8, :].rearrange("p h c -> p (h c)")) for k in range(2)])
                p4 = pv[:, 0:512].rearrange("p (hp two c) -> p hp two c", two=2, c=64)
                v4 = vs[:, half * 8:(half + 1) * 8, :].rearrange("p (hp two) c -> p hp two c", two=2)
                S.copy(v4[:, :, 0, 0:64], p4[:, :, 0, :], e='act')
                S.copy(v4[:, :, 1, 64:128], p4[:, :, 1, :], e='act')
                yield
            S.dma(g.Vtok[a0 + u * 128:a0 + (u + 1) * 128], vs[:], q='pool')

        def q_chain(i, u, rope, qt_, cqn):
            us = slice(u * 128, (u + 1) * 128)
            for third in range(3):
                pq = g.psum()
                S.mmg(pq[:, 0:512], [(cqn[:, k, us], Wuq[:, k, third * 512:(third + 1) * 512]) for k in range(4)])
                S.copy(src[i][:].rearrange("p h c -> p (h c)")[:, third * 512:(third + 1) * 512], pq[:, 0:512],
                       e=('act' if third % 2 else 'dve'))
            yield
            yield from nr_tok(i, 0, rope, qt_, u)

        x2v = g.x2T.rearrange("(k p) t -> p k t", p=128)
        tiles = []
        for seq in g.seqs:
            for ti in range(seq['T'] // TT):
                tiles.append((seq, ti))

        def prologue(n):
            seq, ti = tiles[n]
            o_, r = seq['o'], seq['r']
            t0 = ti * TT
            a0 = o_ + t0
            b = n % 2
            x = xt[b]; h = hT[b]; tmp = tmpb[b]; rst = rstb[b]
            ckf = ckfb[b]; ckn = cknb[b]; cqf = cqfb[b]; cqn = cqnb[b]; rs2 = rs2b[b]
            cks = tmp[:, 0:2, :]; cqs = tmp[:, 2:6, :]
            S.dma(x[:], x2v[:, :, a0:a0 + TT])
            is_lat = (r == 0)
            do_q = is_lat and (t0 < T // 2 + 2)
            if is_lat:
                S.dma(cs[b][:], g.I['ropeCS'][t0:t0 + TT].rearrange("(u p) a c -> p u a c", p=128))
            yield
            S.act(tmp[:], x[:], AF.Square)
            yield
            pp_ = g.psum()
            S.mmg(pp_[:, :TT], [(g.ones[:], tmp[:, k, :]) for k in range(8)])
            S.ts(rst[:], pp_[:, :TT], 1.0 / D, EPS, ALU.mult, ALU.add)
            S.recip(rst[:], rst[:])
            yield
            S.act(rst[:], rst[:], AF.Sqrt)
            yield
            S.tt(tmp[:], x[:], rst[:].unsqueeze(1).to_broadcast([128, 8, TT]), ALU.mult)
            yield
            for c in range(8):
                S.act(h[:, c, :], tmp[:, c, :], AF.Identity, bias=g.mod[:, 1, c, r:r + 1], scale=g.mod[:, 1, 8 + c, r:r + 1])
            yield
            for c in range(2):
                pc = g.psum()
                S.mmg(pc[:, :TT], [(Win[:, k, 512 + c * 128:512 + (c + 1) * 128], h[:, k, :]) for k in range(8)])
                S.copy(ckf[:, c, :], pc[:, :TT], e='act')
            yield
            S.act(cks[:], ckf[:], AF.Square)
            yield
            pst = g.psum()
            S.mmg(pst[:, :TT], [(g.ones[:], cks[:, c, :]) for c in range(2)])
            S.ts(rs2[:], pst[:, :TT], 1.0 / 256, EPS, ALU.mult, ALU.add)
            S.recip(rs2[:], rs2[:])
            yield
            S.act(rs2[:], rs2[:], AF.Sqrt)
            yield
            S.tt(ckf[:], ckf[:], rs2[:].unsqueeze(1).to_broadcast([128, 2, TT]), ALU.mult)
            for c in range(2):
                S.ts(ckn[:, c, :], ckf[:, c, :], kvn[:, c:c + 1], None, ALU.mult)
            yield
            if do_q:
                for c in range(4):
                    pc = g.psum()
                    S.mmg(pc[:, :TT], [(Win[:, k, c * 128:(c + 1) * 128], h[:, k, :]) for k in range(8)])
                    S.copy(cqf[:, c, :], pc[:, :TT], e='act')
                yield
                S.act(cqs[:], cqf[:], AF.Square)
                yield
                pst = g.psum()
                S.mmg(pst[:, :TT], [(g.ones[:], cqs[:, c, :]) for c in range(4)])
                S.ts(rs2[:], pst[:, :TT], 1.0 / 512, EPS, ALU.mult, ALU.add)
                S.recip(rs2[:], rs2[:])
                yield
                S.act(rs2[:], rs2[:], AF.Sqrt)
                yield
                S.tt(cqf[:], cqf[:], rs2[:].unsqueeze(1).to_broadcast([128, 4, TT]), ALU.mult)
                for c in range(4):
                    S.ts(cqn[:, c, :], cqf[:, c, :], qn_[:, c:c + 1], None, ALU.mult)

        def tile_chains(n):
            seq, ti = tiles[n]
            o_, r = seq['o'], seq['r']
            t0 = ti * TT
            a0 = o_ + t0
            b = n % 2
            is_lat = (r == 0)
            do_q = is_lat and (t0 < T // 2 + 2)
            rope = cs[b] if is_lat else None
            kt_ = KTt[b]; qt_ = QTt[b]
            chains = [k_chain(u, u, hT[b], rope, kt_, a0, cknb[b]) for u in range(NU)]
            if do_q:
                chains += [q_chain(2 + u, u, rope, qt_, cqnb[b]) for u in range(NU)]
            return chains, (a0, t0, do_q, kt_, qt_)

        run_zip([prologue(0)])
        for n in range(len(tiles)):
            chains, (a0, t0, do_q, kt_, qt_) = tile_chains(n)
            if n + 1 < len(tiles):
                chains.append(prologue(n + 1))
            run_zip(chains)
            S.dma(g.KT[:, :, a0:a0 + TT].rearrange("h d t -> d h t"), kt_[0:96], q='pool')
            if do_q:
                S.dma(g.QT[:, :, t0:t0 + TT].rearrange("h d t -> d h t"), qt_[0:96], q='pool')


def phase_l1b(g):
    S = g.S
    T, TC, TA = g.T, g.TC, g.TA
    NKT = TA // 128
    QW = 512
    qtiles = [(q0, QW) for q0 in range(0, T // 2, QW)] + [(T // 2, 2)]
    scale = 96.0 ** -0.5
    with Phase(S) as ph:
        KT = [ph.sb('KT%d' % i, [128, TA], BF16) for i in range(2)]
        V = [ph.sb('V%d' % i, [128, NKT, 128], BF16) for i in range(2)]
        Q = [ph.sb('Q%d' % i, [128, QW], BF16) for i in range(2)]
        for i in range(2):
            S.memset(KT[i][96:128, :], 0.0); S.memset(Q[i][96:128, :], 0.0)
        P = [ph.sb('P%d' % i, [128, 2, QW], BF16) for i in range(3)]
        osb = [ph.sb('osb%d' % i, [128, QW], F32) for i in range(2)]
        oo = [ph.sb('oo%d' % i, [128, QW], BF16) for i in range(2)]
        shA = ph.sb('shA', [128, 128], F32); shB = ph.sb('shB', [128, 128], F32)
        S.memset(shA[:], 0.0); S.memset(shB[:], 0.0)
        S.copy(shA[64:128, 0:64], g.ident[64:128, 64:128], e='pool')
        S.copy(shB[0:64, 64:128], g.ident[0:64, 0:64], e='pool')
        qi = 0
        pi = 0
        si = 0
        for h in range(16):
            kt_sb = KT[h % 2]; v_sb = V[h % 2]
            S.dma(kt_sb[0:96, :], g.KT[h])
            S.dma(v_sb[:], g.Vtok[:, h, :].rearrange("(n p) c -> p n c", p=128))
            odd = h % 2
            for (q0, w) in qtiles:
                q = Q[qi % 2]; ob = osb[qi % 2]; o2 = oo[qi % 2]
                qi += 1
                S.dma(q[0:96, 0:w], g.QT[h, :, q0:q0 + w])
                pO = g.ps[0]
                LA = 2
                psl = {}
                NP = NKT // 2

                def issue_s(kp):
                    nonlocal si
                    pdb = g.pd[1 + si % 3]
                    si += 1
                    for j in range(2):
                        kt = 2 * kp + j
                        S.mm(pdb[:, j * 512:j * 512 + w], kt_sb[:, kt * 128:(kt + 1) * 128], q[:, 0:w])
                    psl[kp] = pdb
                for kp in range(min(LA, NP)):
                    issue_s(kp)
                for kp in range(NP):
                    if kp + LA < NP:
                        issue_s(kp + LA)
                    pdb = psl.pop(kp)
                    p_ = P[pi % 3]
                    pi += 1
                    S.act(p_[:, :, 0:w], pdb[:, :].rearrange("p (b c) -> p b c", b=2)[:, :, 0:w], AF.Exp, scale=scale)
                    for j in range(2):
                        kt = 2 * kp + j
                        S.op('pe', lambda E, p_=p_, kt=kt, j=j: E.matmul(pO[:, 0:w], lhsT=v_sb[:, kt, :], rhs=p_[:, j, 0:w],
                                                                         start=(kt == 0), stop=(kt == NKT - 1)),
                             [v_sb[:, kt, :], p_[:, j, 0:w]], [pO[:, 0:w]], inc=True)
                S.copy(ob[:, 0:w], pO[:, 0:w], e='act')
                dr = slice(0, 64) if odd else slice(64, 128)
                nr_ = slice(64, 128) if odd else slice(0, 64)
                S.recip(ob[dr, 0:w], ob[dr, 0:w])
                pb = g.ps[1]
                S.mmg(pb[:, 0:w], [((shB if odd else shA)[:], ob[:, 0:w])])
                S.tt(o2[nr_, 0:w], ob[nr_, 0:w], pb[nr_, 0:w], ALU.mult)
                S.dma(g.OT[h * 64:(h + 1) * 64, q0:q0 + w], o2[nr_, 0:w], q='pool')


def phase_l1c(g):
    S = g.S
    T, TC = g.T, g.TC
    NQ = T // 2 + 2
    with Phase(S) as ph:
        stg = [ph.sb('stg%d' % i, [128, 2048], F32) for i in range(2)]
        Wmo = ph.sb('Wmo', [128, 8, 1024], BF16)
        load_w(g, ph, Wmo, g.I['mla_w_out'], 1024, 1024, stg)
        ot = [ph.sb('ot%d' % i, [128, 8, TT], BF16) for i in range(2)]
        xt = [ph.sb('xt%d' % i, [128, 8, TT], F32) for i in range(2)]
        otv = g.OT.rearrange("(k p) t -> p k t", p=128)
        x2v = g.x2T.rearrange("(k p) t -> p k t", p=128)
        x3v = g.x3T.rearrange("(k p) t -> p k t", p=128)
        it = 0
        for t0 in range(0, NQ, TT):
            w = min(TT, NQ - t0)
            b = it % 2
            it += 1
            S.dma(ot[b][:, :, 0:w], otv[:, :, t0:t0 + w])
            S.dma(xt[b][:, :, 0:w], x2v[:, :, TC + t0:TC + t0 + w])
            for oc in range(8):
                py = g.psum()
                S.mmg(py[:, :w], [(Wmo[:, k, oc * 128:(oc + 1) * 128], ot[b][:, k, 0:w]) for k in range(8)])
                S.stt(xt[b][:, oc, 0:w], py[:, :w], g.mod[:, 1, 16 + oc, 0:1], xt[b][:, oc, 0:w], ALU.mult, ALU.add)
            S.dma(x3v[:, :, t0:t0 + w], xt[b][:, :, 0:w], q='pool')


def _consts():
    f = np.float32
    m = {}
    m['c_ident'] = np.eye(128, dtype=f)
    m['c_ones'] = np.ones((128, 128), f)
    m['c_triu'] = np.triu(np.ones((128, 128), f))
    m['c_tril'] = np.tril(np.ones((128, 128), f))
    return m


def _rope(T):
    f = np.float32
    rows = T // 64
    row = np.repeat(np.arange(rows, dtype=f), 64)
    col = np.tile(np.arange(64, dtype=f), rows)
    half = 16
    inv = (f(10000.0) ** (-np.arange(0, half, 2, dtype=f) / f(half))).astype(f)
    ang = np.concatenate([row[:, None] * inv, col[:, None] * inv], axis=-1).astype(f)
    c = np.cos(ang).astype(f); s = np.sin(ang).astype(f)
    return np.stack([c, s], axis=1)


def _pp(v, nch):
    return np.ascontiguousarray(np.asarray(v, np.float32).reshape(nch, 128).T)


def prep(inp, b, s, T):
    f = np.float32
    rev = (s == 1)
    A = lambda a: np.ascontiguousarray(np.asarray(a, f))
    x = np.asarray(inp['x'][b], f); ctx = np.asarray(inp['ctx'][b], f)
    if rev:
        x = x[::-1]; ctx = ctx[::-1]
    m = dict(_consts())
    m['xT'] = A(x.T); m['ctxT'] = A(ctx.T)
    cc = np.stack([np.asarray(inp['c'][b], f), np.asarray(inp['c_ctx'], f)], axis=-1)
    m['cc'] = A(cc.reshape(8, 128, 2).transpose(1, 0, 2))
    m['ada_w'] = A(inp['ada_w'])
    m['ada_b'] = A(np.asarray(inp['ada_b'], f).reshape(2, 48, 128).transpose(2, 0, 1))
    w = np.array(inp['ab_w_in'][0], f)
    gbias = np.array(inp['ab_gate_bias'][0], f)
    cw = np.asarray(inp['ab_conv_w'][0], f)
    if rev:
        w = np.concatenate([w[:, :2560], w[:, 2568:2576], w[:, 2560:2568]], axis=1)
        gbias = np.concatenate([gbias[8:16], gbias[0:8]])
        cw = cw[::-1]
    m['ab_w_in'] = A(w)
    m['gate_bias'] = A(np.broadcast_to(gbias, (128, 16)))
    m['conv_w'] = A(cw.T.reshape(4, 128, 31).transpose(1, 0, 2))
    vec = np.stack([np.asarray(inp[k][0], f) for k in ('ab_conv_b', 'ab_ln_g', 'ab_ln_b', 'ab_head_gain')])
    m['ab_vec'] = A(vec.reshape(4, 4, 128).transpose(2, 0, 1))
    m['ab_w_out'] = A(inp['ab_w_out'][0])
    m['ffn_w_in'] = A(inp['ffn_w_in'])
    fcw = np.asarray(inp['ffn_conv_w'], f)
    if rev:
        fcw = fcw[:, ::-1]
    m['ffn_cw'] = A(fcw.reshape(2, 3, 22, 128).transpose(3, 0, 2, 1))
    m['ffn_cb'] = A(np.asarray(inp['ffn_conv_b'], f).reshape(2, 22, 128).transpose(2, 0, 1))
    m['ffn_w_out'] = A(inp['ffn_w_out'])
    m['mla_w_in'] = A(inp['mla_w_in'][0])
    m['mla_qn'] = _pp(inp['mla_q_norm'][0], 4)
    m['mla_kvn'] = _pp(inp['mla_kv_norm'][0], 2)
    m['mla_w_uq'] = A(inp['mla_w_uq'][0]); m['mla_w_ukv'] = A(inp['mla_w_ukv'][0])
    gn = np.stack([np.asarray(inp['mla_q_gain'][0], f), np.asarray(inp['mla_k_gain'][0], f)], axis=0)
    m['mla_gain_rep'] = A(np.broadcast_to(gn, (128, 2, 96)))
    m['mla_w_out'] = A(inp['mla_w_out'][0])
    cs = _rope(T)
    if rev:
        cs = cs[::-1]
    m['ropeCS'] = A(cs)
    return m


_NC_CACHE = {}


def kernel(**inputs):
    T = inputs['x'].shape[1]
    B = inputs['x'].shape[0]
    if T not in _NC_CACHE:
        _NC_CACHE[T] = build(T)
    nc = _NC_CACHE[T]
    maps = []
    for b in range(B):
        for s in range(2):
            maps.append(prep(inputs, b, s, T))
    res = run_bass_kernel_spmd(nc, maps, core_ids=list(range(2 * B)))
    out = np.empty((B, T, D), np.float32)
    for b in range(B):
        for s in range(2):
            o = res.results[2 * b + s]['outT'].T
            if s == 0:
                out[b, :T // 2] = o
            else:
                out[b, T // 2:] = o[::-1]
    return out
```

```python
import numpy as np
from concourse.bass_utils import run_bass_kernel_spmd
from contextlib import ExitStack
import concourse.bass as bass
import concourse.mybir as mybir

F32 = mybir.dt.float32
BF16 = mybir.dt.bfloat16
AF = mybir.ActivationFunctionType
ALU = mybir.AluOpType


def _strides(shape):
    st = [1] * len(shape)
    for i in range(len(shape) - 2, -1, -1):
        st[i] = st[i + 1] * shape[i + 1]
    return st


class Sched:
    NDMA = 48

    def __init__(self, nc, es):
        self.nc = nc
        self.E = {'pe': nc.tensor, 'act': nc.scalar, 'dve': nc.vector, 'pool': nc.gpsimd, 'sp': nc.sync}
        self.sem = {k: es.enter_context(nc.semaphore('s_' + k)) for k in self.E}
        self.cnt = {k: 0 for k in self.E}
        self.seen = {k: {} for k in self.E}
        self.dsem = [es.enter_context(nc.semaphore('d%d' % i)) for i in range(self.NDMA)]
        self.dval = [0] * self.NDMA
        self.dn = 0
        self.dn_sw = 0
        self.acc = {}
        self.shapes = {}
        self.psum_names = set()
        self.nins = 0

    def box(self, ap):
        t = ap.tensor
        name = t.name
        shape = list(t.shape)
        st = _strides(shape)
        off = ap.offset
        lo = []
        for k in range(len(shape)):
            lo.append(off // st[k])
            off = off % st[k]
        hi = list(lo)
        for (step, count) in ap.ap:
            if count <= 1 or step == 0:
                continue
            step = abs(step)
            kk = None
            for k in range(len(shape)):
                if st[k] <= step and step % st[k] == 0:
                    kk = k
                    break
            if kk is None:
                return name, None
            hi[kk] += (step // st[kk]) * (count - 1)
            if hi[kk] >= shape[kk]:
                return name, None
        if name in self.psum_names:
            return name, ((lo[-1] // 512, hi[-1] // 512),)
        return name, tuple(zip(lo, hi))

    @staticmethod
    def _ov(a, b):
        if a is None or b is None:
            return True
        for (l1, h1), (l2, h2) in zip(a, b):
            if h1 < l2 or h2 < l1:
                return False
        return True

    @staticmethod
    def _inside(a, b):
        if b is None:
            return True
        if a is None:
            return False
        for (l1, h1), (l2, h2) in zip(a, b):
            if l1 < l2 or h1 > h2:
                return False
        return True

    def _wait(self, e, tok):
        kind = tok[0]
        if kind == 'dma':
            key = ('dma', tok[1])
            val = tok[2]
            sem = self.dsem[tok[1]]
        else:
            if kind == 'pe' and e == 'pe':
                return
            key = kind
            val = tok[1]
            sem = self.sem[kind]
        if self.seen[e].get(key, 0) >= val:
            return
        self.E[e].wait_ge(sem, val)
        self.nins += 1
        self.seen[e][key] = val

    def _deps(self, e, reads, writes):
        toks = set()
        rb = [self.box(a) for a in reads]
        wb = [self.box(a) for a in writes]
        for name, bx in rb:
            isp = name in self.psum_names
            for (b2, tok, isw, e2) in self.acc.get(name, ()):
                if (isw or (isp and e2 != e)) and self._ov(bx, b2):
                    toks.add(tok)
        for name, bx in wb:
            for (b2, tok, isw, e2) in self.acc.get(name, ()):
                if self._ov(bx, b2):
                    toks.add(tok)
        for tok in sorted(toks, key=str):
            self._wait(e, tok)
        return rb, wb

    def _record(self, e, tok, rb, wb):
        for name, bx in wb:
            lst = self.acc.setdefault(name, [])
            lst[:] = [x for x in lst if not self._inside(x[0], bx)]
            lst.append((bx, tok, True, e))
        for name, bx in rb:
            lst = self.acc.setdefault(name, [])
            if tok[0] != 'dma':
                lst[:] = [x for x in lst if not ((not x[2]) and x[3] == e and x[0] == bx and x[1][0] != 'dma')]
            lst.append((bx, tok, False, e))

    def op(self, e, fn, reads, writes, inc=True):
        rb, wb = self._deps(e, reads, writes)
        ins = fn(self.E[e])
        self.nins += 1
        if inc:
            self.cnt[e] += 1
            ins.then_inc(self.sem[e], 1)
            tok = (e, self.cnt[e])
        else:
            tok = (e, self.cnt[e] + 1)
        self._record(e, tok, rb, wb)
        return ins

    def dma(self, out, in_, q='sp'):
        rb, wb = self._deps(q, [in_], [out])
        half = self.NDMA // 2
        if q == 'pool':
            k = half + self.dn_sw % half
            self.dn_sw += 1
        else:
            k = self.dn % half
            self.dn += 1
        if self.dval[k] > 0:
            self._wait(q, ('dma', k, self.dval[k]))
        ins = self.E[q].dma_start(out=out, in_=in_)
        self.nins += 1
        self.dval[k] += 16
        ins.then_inc(self.dsem[k], 16)
        tok = ('dma', k, self.dval[k])
        self._record(q, tok, rb, wb)

    def barrier(self):
        for e in self.E:
            for e2 in self.E:
                if e2 != e and self.cnt[e2] > 0:
                    self._wait(e, (e2, self.cnt[e2]))
            if e != 'pe' and self.cnt[e] > 0:
                self._wait(e, (e, self.cnt[e]))
            for k in range(self.NDMA):
                if self.dval[k] > 0:
                    self._wait(e, ('dma', k, self.dval[k]))
        self.acc = {}

    def mm(self, out, lhsT, rhs, start=True, stop=True):
        return self.op('pe', lambda E: E.matmul(out, lhsT=lhsT, rhs=rhs, start=start, stop=stop),
                       [lhsT, rhs], [out], inc=stop)

    def mmg(self, out, pairs):
        n = len(pairs)
        for i, (l, r) in enumerate(pairs):
            self.mm(out, l, r, start=(i == 0), stop=(i == n - 1))

    def act(self, out, in_, func, bias=None, scale=None, accum_out=None, e='act'):
        kw = {}
        reads = [in_]
        writes = [out]
        if bias is not None:
            kw['bias'] = bias
            if not isinstance(bias, (int, float)):
                reads.append(bias)
        if scale is not None:
            kw['scale'] = scale
            if not isinstance(scale, (int, float)):
                reads.append(scale)
        if accum_out is not None:
            kw['accum_out'] = accum_out
            writes.append(accum_out)
        return self.op(e, lambda E: E.activation(out=out, in_=in_, func=func, **kw), reads, writes)

    def tt(self, out, in0, in1, op, e='dve'):
        return self.op(e, lambda E: E.tensor_tensor(out=out, in0=in0, in1=in1, op=op), [in0, in1], [out])

    def ts(self, out, in0, s1, s2, op0, op1=None, e='dve'):
        reads = [in0]
        for s in (s1, s2):
            if s is not None and not isinstance(s, (int, float)):
                reads.append(s)
        kw = {}
        if op1 is not None:
            kw['op1'] = op1
        return self.op(e, lambda E: E.tensor_scalar(out=out, in0=in0, scalar1=s1, scalar2=s2, op0=op0, **kw),
                       reads, [out])

    def stt(self, out, in0, scalar, in1, op0, op1, e='dve'):
        reads = [in0, in1]
        if not isinstance(scalar, (int, float)):
            reads.append(scalar)
        return self.op(e, lambda E: E.scalar_tensor_tensor(out=out, in0=in0, scalar=scalar, in1=in1, op0=op0, op1=op1),
                       reads, [out])

    def copy(self, out, in_, e='dve'):
        if e == 'act':
            return self.op(e, lambda E: E.activation(out=out, in_=in_, func=AF.Copy), [in_], [out])
        return self.op(e, lambda E: E.tensor_copy(out=out, in_=in_), [in_], [out])

    def memset(self, ap, val, e='pool'):
        return self.op(e, lambda E: E.memset(ap, val), [], [ap])

    def recip(self, out, in_):
        return self.op('dve', lambda E: E.reciprocal(out=out, in_=in_), [in_], [out])


class Phase:
    _n = 0

    def __init__(self, S):
        self.S = S
        self.es = ExitStack()
        Phase._n += 1
        self.pfx = 'p%d_' % Phase._n

    def __enter__(self):
        self.es.__enter__()
        return self

    def sb(self, name, shape, dt):
        return self.es.enter_context(self.S.nc.sbuf_tensor(self.pfx + name, list(shape), dt))

    def __exit__(self, *a):
        self.S.barrier()
        return self.es.__exit__(*a)
TT = 256
EPS = 1e-6
D = 1024


class G:
    pass


def run_zip(gens):
    gens = list(gens)
    while gens:
        nxt = []
        for ge in gens:
            try:
                next(ge)
                nxt.append(ge)
            except StopIteration:
                pass
        gens = nxt


def build(T, TC=256, dbg=(), upto='all'):
    nc = bass.Bass("TRN2", target_bir_lowering=False)
    es = ExitStack()
    with es:
        S = Sched(nc, es)
        g = G()
        g.nc, g.S, g.T, g.TC, g.TA = nc, S, T, TC, T + TC
        g.dbg = set(dbg)
        g.I = {}

        def inp(name, shape, dt=F32):
            g.I[name] = nc.dram_tensor(name, list(shape), dt, kind="ExternalInput").ap()
            return g.I[name]

        def scratch(name, shape, dt=F32):
            kind = "ExternalOutput" if name in g.dbg else "Internal"
            return nc.dram_tensor(name, list(shape), dt, kind=kind).ap()
        g.scratch = scratch
        TA = g.TA
        inp('xT', [D, T]); inp('ctxT', [D, TC]); inp('cc', [128, 8, 2])
        inp('ada_w', [2, D, 6 * D]); inp('ada_b', [128, 2, 48])
        inp('ab_w_in', [D, 2576]); inp('gate_bias', [128, 16]); inp('conv_w', [128, 4, 31])
        inp('ab_vec', [128, 4, 4])
        inp('ab_w_out', [D, D])
        inp('ffn_w_in', [2, D, 5632]); inp('ffn_cw', [128, 2, 22, 3]); inp('ffn_cb', [128, 2, 22])
        inp('ffn_w_out', [2, 2816, D])
        inp('mla_w_in', [D, 800]); inp('mla_qn', [128, 4]); inp('mla_kvn', [128, 2])
        inp('mla_w_uq', [512, 1536]); inp('mla_w_ukv', [256, 2048]); inp('mla_gain_rep', [128, 2, 96])
        inp('mla_w_out', [D, D])
        inp('ropeCS', [T, 2, 16])
        inp('c_ident', [128, 128]); inp('c_ones', [128, 128]); inp('c_triu', [128, 128]); inp('c_tril', [128, 128])
        g.outT = nc.dram_tensor('outT', [D, T // 2], F32, kind="ExternalOutput").ap()

        g.ps = []
        g.pd = []
        for i in range(4):
            t = es.enter_context(nc.psum_tensor('pd%d' % i, [128, 1024], F32))
            S.psum_names.add('pd%d' % i)
            g.pd.append(t)
            g.ps.append(t[:, 0:512]); g.ps.append(t[:, 512:1024])
        g.psi = 0

        def psum():
            t = g.ps[g.psi % 8]
            g.psi += 1
            return t
        g.psum = psum

        def gsb(name, shape, dt):
            return es.enter_context(nc.sbuf_tensor('g_' + name, list(shape), dt))
        g.ident = gsb('ident', [128, 128], F32); g.ones = gsb('ones', [128, 128], F32)
        g.identb = gsb('identb', [128, 128], BF16); g.onesb = gsb('onesb', [128, 128], BF16)
        g.mod = gsb('mod', [128, 2, 48, 2], F32)
        S.dma(g.ident[:], g.I['c_ident']); S.dma(g.ones[:], g.I['c_ones'])
        S.copy(g.identb[:], g.ident[:], e='pool'); S.copy(g.onesb[:], g.ones[:], e='pool')

        g.gluT = scratch('gluT', [512, TA], BF16)
        g.sigoT = scratch('sigoT', [512, TA], F32)
        g.qk32 = scratch('qk32', [TA, 512], F32)
        g.vtok = scratch('vtok', [TA, 512], BF16)
        g.gates = scratch('gates', [TA, 16], F32)
        g.hseq = [scratch('hseq%d' % d, [4, 128, TA], F32) for d in range(2)]
        g.x1T = scratch('x1T', [D, TA], F32)
        g.x2T = scratch('x2T', [D, TA], F32)
        g.x3T = scratch('x3T', [D, T // 2 + 2], F32)
        g.KT = scratch('KT', [16, 96, TA], BF16)
        g.Vtok = scratch('Vtok', [TA, 16, 128], BF16)
        g.QT = scratch('QT', [16, 96, T // 2 + TT], BF16)
        g.OT = scratch('OT', [D, T // 2 + 2], BF16)

        g.seqs = [dict(name='ctx', T=TC, o=0, r=1, xT=g.I['ctxT']), dict(name='lat', T=T, o=TC, r=0, xT=g.I['xT'])]

        phase_ada(g)
        if upto == 'ada':
            dump_mod(g)
        else:
            phase_l0a(g)
            if upto != 'l0a':
                phase_l0b(g)
            if upto not in ('l0a', 'l0b'):
                phase_l0c(g)
            if upto not in ('l0a', 'l0b', 'l0c'):
                TC = g.TC
                phase_ffn(g, 0, [dict(src=g.x1T, c0=0, c1=TC, v0=0, v1=TC, r=1, dst=g.x2T, d0=0),
                                 dict(src=g.x1T, c0=TC, c1=TA, v0=TC, v1=TA, r=0, dst=g.x2T, d0=TC)])
            if upto not in ('l0a', 'l0b', 'l0c', 'l0d'):
                phase_l1a(g)
            if upto not in ('l0a', 'l0b', 'l0c', 'l0d', 'l1a'):
                phase_l1b(g)
            if upto not in ('l0a', 'l0b', 'l0c', 'l0d', 'l1a', 'l1b'):
                phase_l1c(g)
            if upto not in ('l0a', 'l0b', 'l0c', 'l0d', 'l1a', 'l1b', 'l1c'):
                phase_ffn(g, 1, [dict(src=g.x3T, c0=0, c1=T // 2, v0=0, v1=T // 2 + 1, r=0, dst=g.outT, d0=0)])
        S.barrier()
    return nc


def dump_mod(g):
    S = g.S
    o = g.nc.dram_tensor('modout', [128, 2 * 48 * 2], F32, kind="ExternalOutput").ap()
    S.dma(o, g.mod[:].rearrange("p a b c -> p (a b c)"))


def load_w(g, ph, dst, src, K, N, stg, ranges=None, q='pool', ce=('pool', 'dve', 'act')):
    S = g.S
    v = src.rearrange("(k p) n -> p k n", p=128)
    SW = stg[0].shape[1]
    if ranges is None:
        ranges = [(0, N)]
    i = g.__dict__.setdefault('_lw', 0)
    for (r0, r1) in ranges:
        for n0 in range(r0, r1, SW):
            n1 = min(r1, n0 + SW)
            for k in range(K // 128):
                st = stg[i % len(stg)]
                i += 1
                S.dma(st[:, :n1 - n0], v[:, k, n0:n1], q=q)
                S.copy(dst[:, k, n0:n1], st[:, :n1 - n0], e=ce[i % len(ce)])
    g._lw = i


def phase_ada(g):
    S = g.S
    with Phase(S) as ph:
        cc = ph.sb('cc', [128, 8, 2], F32)
        sc = ph.sb('sc', [128, 8, 2], F32)
        adab = ph.sb('adab', [128, 2, 48], F32)
        wst = [ph.sb('adaw%d' % i, [128, 8, 512], F32) for i in range(2)]
        S.dma(cc[:], g.I['cc'])
        S.dma(adab[:], g.I['ada_b'])
        S.act(sc[:], cc[:], AF.Silu)
        for l in range(2):
            wv = g.I['ada_w'][l].rearrange("(k p) n -> p k n", p=128)
            for gi in range(12):
                w = wst[gi % 2]
                S.dma(w[:], wv[:, :, gi * 512:(gi + 1) * 512])
                for jj in range(4):
                    j = gi * 4 + jj
                    p = g.psum()
                    S.mmg(p[:, 0:2], [(w[:, k, jj * 128:(jj + 1) * 128], sc[:, k, :]) for k in range(8)])
                    S.ts(g.mod[:, l, j, :], p[:, 0:2], adab[:, l, j:j + 1], None, ALU.add)
            for m in (1, 4):
                S.ts(g.mod[:, l, m * 8:(m + 1) * 8, :], g.mod[:, l, m * 8:(m + 1) * 8, :], 1.0, None, ALU.add)


def rstd(S, out, in_, mul):
    S.ts(out, in_, mul, EPS, ALU.mult, ALU.add)
    S.recip(out, out)
    S.act(out, out, AF.Sqrt)


def modulate(g, xt, hT, W, l, m_shift, m_scale, r, sq, tn, rst):
    S = g.S
    S.act(sq[:, :, :W], xt, AF.Square)
    p = g.psum()
    ones = g.onesb if sq.dtype == BF16 else g.ones
    S.mmg(p[:, :W], [(ones[:], sq[:, k, :W]) for k in range(8)])
    rstd(S, rst[:, :W], p[:, :W], 1.0 / D)
    S.tt(tn[:, :, :W], xt, rst[:, :W].unsqueeze(1).to_broadcast([128, 8, W]), ALU.mult)
    for c in range(8):
        S.act(hT[:, c, :W], tn[:, c, :W], AF.Identity,
              bias=g.mod[:, l, m_shift * 8 + c, r:r + 1], scale=g.mod[:, l, m_scale * 8 + c, r:r + 1])


def modulate_g(g, xt, hT, W, l, m_shift, m_scale, r, sq, tn, rst):
    S = g.S
    S.act(sq[:, :, :W], xt, AF.Square)
    yield
    p = g.psum()
    ones = g.onesb if sq.dtype == BF16 else g.ones
    S.mmg(p[:, :W], [(ones[:], sq[:, k, :W]) for k in range(8)])
    S.ts(rst[:, :W], p[:, :W], 1.0 / D, EPS, ALU.mult, ALU.add)
    S.recip(rst[:, :W], rst[:, :W])
    yield
    S.act(rst[:, :W], rst[:, :W], AF.Sqrt)
    yield
    S.tt(tn[:, :, :W], xt, rst[:, :W].unsqueeze(1).to_broadcast([128, 8, W]), ALU.mult)
    yield
    for c in range(8):
        S.act(hT[:, c, :W], tn[:, c, :W], AF.Identity,
              bias=g.mod[:, l, m_shift * 8 + c, r:r + 1], scale=g.mod[:, l, m_scale * 8 + c, r:r + 1])
    yield


def phase_l0a(g):
    S = g.S
    with Phase(S) as ph:
        Wab = ph.sb('Wab', [128, 8, 2576], BF16)
        stg = [ph.sb('stg%d' % i, [128, 2048], F32) for i in range(2)]
        load_w(g, ph, Wab, g.I['ab_w_in'], 1024, 2576, stg)
        gb = ph.sb('gb', [128, 16], F32)
        S.dma(gb[:], g.I['gate_bias'])
        xt = [ph.sb('xt%d' % i, [128, 8, TT], F32) for i in range(2)]
        sqb = [ph.sb('sq%d' % i, [128, 8, TT], F32) for i in range(2)]
        rstb = [ph.sb('rst%d' % i, [128, TT], F32) for i in range(2)]
        sqh = [ph.sb('sqh%d' % i, [128, 8, TT], BF16) for i in range(2)]
        hT = [ph.sb('hT%d' % i, [128, 8, TT], BF16) for i in range(2)]
        sig = [ph.sb('sig%d' % i, [128, TT], F32) for i in range(4)]
        glu = [ph.sb('glu%d' % i, [128, 4, TT], BF16) for i in range(2)]
        so = [ph.sb('so%d' % i, [128, 4, TT], F32) for i in range(2)]
        qk32 = [ph.sb('qk32_%d' % i, [128, TT // 128, 512], F32) for i in range(2)]
        vt = [ph.sb('vt%d' % i, [128, TT // 128, 512], BF16) for i in range(2)]
        gt = [ph.sb('gt%d' % i, [128, TT // 128, 16], F32) for i in range(2)]
        tiles = []
        for seq in g.seqs:
            for ti in range(seq['T'] // TT):
                tiles.append((seq, ti))

        def tile(n):
            seq, ti = tiles[n]
            it = n
            xv = seq['xT'].rearrange("(k p) t -> p k t", p=128)
            t0 = ti * TT
            a0 = seq['o'] + t0
            x = xt[it % 2]; h = hT[it % 2]
            S.dma(x[:], xv[:, :, t0:t0 + TT])
            yield
            yield from modulate_g(g, x[:], h, TT, 0, 0, 1, seq['r'], sqh[it % 2], sqb[it % 2], rstb[it % 2])
            gl = glu[it % 2]; sg = so[it % 2]; qk = qk32[it % 2]; vv = vt[it % 2]; gg = gt[it % 2]
            for c in range(4):
                pa = g.psum(); pg = g.psum()
                S.mmg(pa[:, :TT], [(Wab[:, k, c * 128:(c + 1) * 128], h[:, k, :]) for k in range(8)])
                S.mmg(pg[:, :TT], [(Wab[:, k, 512 + c * 128:512 + (c + 1) * 128], h[:, k, :]) for k in range(8)])
                s_ = sig[2 * (it % 2) + c % 2]
                S.act(s_[:], pg[:, :TT], AF.Sigmoid)
                S.tt(gl[:, c, :], pa[:, :TT], s_[:], ALU.mult)
                yield
            S.dma(g.gluT.rearrange("(c p) t -> p c t", p=128)[:, :, a0:a0 + TT], gl[:], q='pool')
            for c in range(4):
                po = g.psum()
                S.mmg(po[:, :TT], [(Wab[:, k, 2048 + c * 128:2048 + (c + 1) * 128], h[:, k, :]) for k in range(8)])
                S.act(sg[:, c, :], po[:, :TT], AF.Sigmoid)
                if c % 2:
                    yield
            S.dma(g.sigoT.rearrange("(c p) t -> p c t", p=128)[:, :, a0:a0 + TT], sg[:], q='pool')
            for u in range(TT // 128):
                hs = slice(u * 128, (u + 1) * 128)
                pq = g.psum(); pv = g.psum(); pgt = g.psum()
                S.mmg(pq[:, 0:512], [(h[:, k, hs], Wab[:, k, 1024:1536]) for k in range(8)])
                S.mmg(pv[:, 0:512], [(h[:, k, hs], Wab[:, k, 1536:2048]) for k in range(8)])
                S.mmg(pgt[:, 0:16], [(h[:, k, hs], Wab[:, k, 2560:2576]) for k in range(8)])
                S.copy(qk[:, u, 0:256], pq[:, 0:256], e='act')
                S.act(qk[:, u, 256:512], pq[:, 256:512], AF.Copy, scale=0.125)
                S.copy(vv[:, u, :], pv[:, 0:512], e='dve')
                S.tt(gg[:, u, :], pgt[:, 0:16], gb[:], ALU.add)
                yield
            S.dma(g.qk32[a0:a0 + TT, :].rearrange("(u p) c -> p u c", p=128), qk[:], q='pool')
            S.dma(g.vtok[a0:a0 + TT, :].rearrange("(u p) c -> p u c", p=128), vv[:], q='pool')
            S.dma(g.gates[a0:a0 + TT, :].rearrange("(u p) c -> p u c", p=128), gg[:], q='pool')

        for n in range(0, len(tiles), 2):
            run_zip([tile(m) for m in range(n, min(n + 2, len(tiles)))])


def phase_l0b(g):
    S = g.S
    NCH = g.TA // 128
    NCC = g.TC // 128
    with Phase(S) as ph:
        triu = ph.sb('triu', [128, 128], F32); tril = ph.sb('tril', [128, 128], F32)
        S.dma(triu[:], g.I['c_triu']); S.dma(tril[:], g.I['c_tril'])
        U = [triu, tril]
        GT = ph.sb('GT', [128, NCH, 16], F32)
        S.dma(GT[:], g.gates.rearrange("(c p) g -> p c g", p=128))
        L1 = ph.sb('L1', [128, 2, NCH, 4], F32)
        Bc = ph.sb('Bc', [128, 2, NCH, 4], F32)
        Ecol = ph.sb('Ecol', [128, 2, NCH, 4], F32)
        Acol = ph.sb('Acol', [128, 2, NCH, 4], F32)
        Wcol = ph.sb('Wcol', [128, 2, NCH, 4], F32)
        DEC = ph.sb('DEC', [128, 2, NCH, 4], F32)
        DECP = ph.sb('DECP', [128, 2, NCH, 2], F32)
        for d in range(2):
            S.act(L1[:, d], GT[:, :, 4 + 8 * d:8 + 8 * d], AF.Exp, scale=-1.0)
            S.act(L1[:, d], L1[:, d], AF.Ln, bias=1.0)
            pb = g.psum()
            S.mmg(pb[:, 0:NCH * 4], [(U[d][:], L1[:, d].rearrange("p c h -> p (c h)"))])
            S.copy(Bc[:, d].rearrange("p c h -> p (c h)"), pb[:, 0:NCH * 4], e='dve')
            pt = g.psum()
            S.mmg(pt[:, 0:NCH * 4], [(g.ones[:], L1[:, d].rearrange("p c h -> p (c h)"))])
            S.act(DEC[:, d].rearrange("p c h -> p (c h)"), pt[:, 0:NCH * 4], AF.Exp, scale=-1.0)
            S.act(Ecol[:, d], Bc[:, d], AF.Exp, scale=-1.0)
            S.tt(Acol[:, d], Bc[:, d], GT[:, :, 8 * d:8 * d + 4], ALU.add)
            S.act(Acol[:, d], Acol[:, d], AF.Exp)
            S.tt(Wcol[:, d], Acol[:, d], DEC[:, d], ALU.mult)
            dv = DEC[:, d].rearrange("p c (q two) -> p c q two", two=2)
            S.copy(DECP[0:64, d], dv[0:64, :, :, 0], e='dve')
            S.copy(DECP[64:128, d], dv[64:128, :, :, 1], e='dve')
        NB = 2
        XQ = [[ph.sb('XQ%d_%d' % (d, i), [128, 512], F32) for i in range(NB)] for d in range(2)]
        X = [[ph.sb('X%d_%d' % (d, i), [128, 514], BF16) for i in range(NB)] for d in range(2)]
        qs = [[ph.sb('qs%d_%d' % (d, i), [128, 4, 128], BF16) for i in range(NB)] for d in range(2)]
        ka = [[ph.sb('ka%d_%d' % (d, i), [128, 4, 64], BF16) for i in range(NB)] for d in range(2)]
        kw = [[ph.sb('kw%d_%d' % (d, i), [128, 4, 64], BF16) for i in range(NB)] for d in range(2)]
        QKT = [[ph.sb('QKT%d_%d' % (d, i), [128, 4, 128], BF16) for i in range(NB)] for d in range(2)]
        KTs = [[ph.sb('KT%d_%d' % (d, i), [128, 2, 128], BF16) for i in range(NB)] for d in range(2)]
        PFs = [[ph.sb('PF%d_%d' % (d, i), [128, 4, 128], F32) for i in range(NB)] for d in range(2)]
        PTs = [[ph.sb('PT%d_%d' % (d, i), [128, 4, 128], BF16) for i in range(NB)] for d in range(2)]
        dn = [[ph.sb('dn%d_%d' % (d, i), [128, 4, 128], F32) for i in range(NB)] for d in range(2)]
        ho = [[ph.sb('ho%d_%d' % (d, i), [128, 4, 128], F32) for i in range(NB)] for d in range(2)]
        St = [[ph.sb('St%d_%d' % (d, p), [128, 257], F32) for p in range(2)] for d in range(2)]
        Sb = [[[ph.sb('Sb%d_%d_%d' % (d, p, i), [128, 256], BF16) for i in range(2)] for p in range(2)] for d in range(2)]
        nr = [[[ph.sb('nr%d_%d_%d' % (d, p, i), [128, 128], BF16) for i in range(2)] for p in range(2)] for d in range(2)]
        for d in range(2):
            for i in range(NB):
                S.memset(X[d][i][:, 256:257], 1.0); S.memset(X[d][i][:, 513:514], 1.0)
                S.memset(qs[d][i][:], 0.0)
            for p in range(2):
                S.memset(St[d][p][:], 0.0)
                S.memset(Sb[d][p][0][:], 0.0)
                S.memset(nr[d][p][0][:], 0.0)
        order = [list(range(NCH)), list(range(NCC - 1, -1, -1)) + list(range(NCH - 1, NCC - 1, -1))]

        def bcol(col, d, c):
            return col[:, d, c, :].unsqueeze(2).to_broadcast([128, 4, 64])

        def part1(s, d):
            c = order[d][s]
            b = s % NB
            x = X[d][b]; xq = XQ[d][b]
            S.dma(xq[:], g.qk32[c * 128:(c + 1) * 128, :])
            S.dma(x[:, 0:256], g.vtok[c * 128:(c + 1) * 128, 0:256])
            S.dma(x[:, 257:513], g.vtok[c * 128:(c + 1) * 128, 256:512])
            yield
            k3 = xq[:, 256:512].rearrange("p (h e) -> p h e", h=4)
            q5 = qs[d][b][:].rearrange("p (pp r) (rr e) -> p pp r rr e", r=2, rr=2)
            q4 = xq[:, 0:256].rearrange("p (pp r e) -> p pp r e", pp=2, r=2)
            for r in range(2):
                S.tt(q5[:, :, r, r, :], q4[:, :, r, :],
                     Ecol[:, d, c, :].rearrange("p (pp r) -> p pp r", r=2)[:, :, r].unsqueeze(2).to_broadcast([128, 2, 64]),
                     ALU.mult)
            S.tt(ka[d][b][:], k3, bcol(Acol, d, c), ALU.mult)
            S.tt(kw[d][b][:], k3, bcol(Wcol, d, c), ALU.mult, e='pool')
            yield
            ptq = g.psum(); ptk = g.psum()
            for h in range(4):
                S.mmg(ptq[:, h * 128:(h + 1) * 128], [(qs[d][b][:, h, :], g.identb[:])])
            for p in range(2):
                S.mmg(ptk[:, p * 128:(p + 1) * 128],
                      [(ka[d][b][:, 2 * p:2 * p + 2, :].rearrange("p h e -> p (h e)"), g.identb[:])])
            QT = QKT[d][b]; KT = KTs[d][b]
            S.copy(QT[:].rearrange("p a n -> p (a n)"), ptq[:, 0:512], e='act')
            S.copy(KT[:].rearrange("p a n -> p (a n)"), ptk[:, 0:256], e='act')
            yield
            pS = g.psum()
            for h in range(4):
                S.mmg(pS[:, h * 128:(h + 1) * 128], [(KT[:, h // 2, :], QT[:, h, :])])
            PT = PTs[d][b]; PF = PFs[d][b]
            S.tt(PF[:], pS[:, 0:512].rearrange("p (h i) -> p h i", h=4),
                 U[d][:].unsqueeze(1).to_broadcast([128, 4, 128]), ALU.mult)
            yield
            S.copy(PT[:], PF[:], e='act')

        def part2(s, d):
            c = order[d][s]
            b = s % NB
            sb = s % 2
            x = X[d][b]; QT = QKT[d][b]; PT = PTs[d][b]; PF = PFs[d][b]
            pN = g.psum(); pD = g.psum()
            for h in range(4):
                p, r = h // 2, h % 2
                base = p * 257
                S.mmg(pN[:, h * 128:(h + 1) * 128],
                      [(x[:, base + r * 128:base + (r + 1) * 128], PT[:, h, :]),
                       (Sb[d][p][sb][:, r * 128:(r + 1) * 128], QT[:, h, :])])
            S.mm(pD[:, 0:512], g.ones[:], PF[:].rearrange("p h i -> p (h i)"), start=True, stop=False)
            for h in range(4):
                S.mm(pD[:, h * 128:(h + 1) * 128], nr[d][h // 2][sb][:], QT[:, h, :], start=False, stop=(h == 3))
            dnn = dn[d][b]; hoo = ho[d][b]
            S.act(dnn[:].rearrange("p h i -> p (h i)"), pD[:, 0:512], AF.Abs)
            S.ts(dnn[:], dnn[:], 1.0, None, ALU.max)
            S.recip(dnn[:], dnn[:])
            S.tt(hoo[:].rearrange("p h i -> p (h i)"), pN[:, 0:512], dnn[:].rearrange("p h i -> p (h i)"), ALU.mult)
            yield
            S.dma(g.hseq[d][:, :, c * 128:(c + 1) * 128].rearrange("h v t -> v h t"), hoo[:], q='pool')
            for p in range(2):
                base = p * 257
                pU = g.psum()
                S.mmg(pU[:, 0:257], [(kw[d][b][:, 2 * p:2 * p + 2, :].rearrange("p h e -> p (h e)"), x[:, base:base + 257])])
                S.stt(St[d][p][:], St[d][p][:], DECP[:, d, c, p:p + 1], pU[:, 0:257], ALU.mult, ALU.add)
                yield
                S.copy(Sb[d][p][1 - sb][:], St[d][p][:, 0:256], e='act')
                S.copy(nr[d][p][1 - sb][:], St[d][p][:, 256:257].to_broadcast([128, 128]), e='act')

        run_zip([part1(0, d) for d in range(2)])
        for s in range(NCH):
            chains = [part2(s, d) for d in range(2)]
            if s + 1 < NCH:
                chains += [part1(s + 1, d) for d in range(2)]
            run_zip(chains)


def phase_l0c(g):
    S = g.S
    with Phase(S) as ph:
        stg = [ph.sb('stg%d' % i, [128, 2048], F32) for i in range(2)]
        Wout = ph.sb('Wout', [128, 8, 1024], BF16)
        load_w(g, ph, Wout, g.I['ab_w_out'], 1024, 1024, stg)
        cw = ph.sb('cw', [128, 4, 31], F32); vec = ph.sb('vec', [128, 4, 4], F32)
        S.dma(cw[:], g.I['conv_w']); S.dma(vec[:], g.I['ab_vec'])
        diagW = ph.sb('diagW', [128, 4, 31, 128], BF16)
        n = 0
        for c in range(4):
            for k in range(31):
                if n % 2:
                    S.act(diagW[:, c, k, :], g.identb[:], AF.Copy, scale=cw[:, c, k:k + 1])
                else:
                    S.ts(diagW[:, c, k, :], g.identb[:], cw[:, c, k:k + 1], None, ALU.mult)
                n += 1
        HW = TT + 30
        GH = [ph.sb('GH%d' % i, [128, 4, HW], BF16) for i in range(2)]
        upreb = [ph.sb('upre%d' % i, [128, 4, TT], F32) for i in range(2)]
        usqb = [ph.sb('usq%d' % i, [128, 4, TT], F32) for i in range(2)]
        meanb = [ph.sb('mean%d' % i, [128, TT], F32) for i in range(2)]
        msqb = [ph.sb('msq%d' % i, [128, TT], F32) for i in range(2)]
        rsb = [ph.sb('rs%d' % i, [128, TT], F32) for i in range(2)]
        hf = [ph.sb('hf%d' % i, [128, 4, TT], F32) for i in range(2)]
        hb = [ph.sb('hb%d' % i, [128, 4, TT], F32) for i in range(2)]
        sgo = [ph.sb('sgo%d' % i, [128, 4, TT], F32) for i in range(2)]
        hsqb = [ph.sb('hsq%d' % i, [128, 4, TT], F32) for i in range(2)]
        rshb = [ph.sb('rsh%d' % i, [128, 4, TT], F32) for i in range(2)]
        cat = [ph.sb('cat%d' % i, [128, 8, TT], BF16) for i in range(2)]
        xt = [ph.sb('xt%d' % i, [128, 8, TT], F32) for i in range(2)]
        gluv = g.gluT.rearrange("(c p) t -> p c t", p=128)
        sigv = g.sigoT.rearrange("(c p) t -> p c t", p=128)
        x1v = g.x1T.rearrange("(k p) t -> p k t", p=128)
        tiles = []
        for seq in g.seqs:
            for ti in range(seq['T'] // TT):
                tiles.append((seq, ti))

        def tile(n):
            seq, ti = tiles[n]
            xv = seq['xT'].rearrange("(k p) t -> p k t", p=128)
            o, Ts, r = seq['o'], seq['T'], seq['r']
            t0 = ti * TT
            a0 = o + t0
            b = n % 2
            upre, usq, mean, msq, rs, hsq, rsh = upreb[b], usqb[b], meanb[b], msqb[b], rsb[b], hsqb[b], rshb[b]
            gh = GH[b]
            lo = max(t0 - 15, 0); hi = min(t0 + TT + 15, Ts)
            if lo > t0 - 15:
                S.memset(gh[:, :, 0:lo - (t0 - 15)], 0.0)
            if hi < t0 + TT + 15:
                S.memset(gh[:, :, hi - (t0 - 15):HW], 0.0)
            S.dma(gh[:, :, lo - (t0 - 15):hi - (t0 - 15)], gluv[:, :, o + lo:o + hi])
            S.dma(hf[b][:], g.hseq[0][:, :, a0:a0 + TT].rearrange("h v t -> v h t"))
            S.dma(hb[b][:], g.hseq[1][:, :, a0:a0 + TT].rearrange("h v t -> v h t"))
            S.dma(sgo[b][:], sigv[:, :, a0:a0 + TT])
            S.dma(xt[b][:], xv[:, :, t0:t0 + TT])
            ct = cat[b]
            yield
            for c in range(4):
                pc = g.psum()
                S.mmg(pc[:, :TT], [(diagW[:, c, k, :], gh[:, c, k:k + TT]) for k in range(31)])
                S.act(upre[:, c, :], pc[:, :TT], AF.Identity, bias=vec[:, 0, c:c + 1])
                yield
            S.act(usq[:], upre[:], AF.Square)
            S.tt(hf[b][:], hf[b][:], hb[b][:], ALU.add, e='pool')
            yield
            S.act(hsq[:], hf[b][:], AF.Square)
            p1 = g.psum(); p2 = g.psum()
            S.mmg(p1[:, :TT], [(g.ones[:], upre[:, c, :]) for c in range(4)])
            S.mmg(p2[:, :TT], [(g.ones[:], usq[:, c, :]) for c in range(4)])
            S.ts(mean[:], p1[:, :TT], 1.0 / 512, None, ALU.mult)
            S.tt(msq[:], mean[:], mean[:], ALU.mult)
            S.stt(rs[:], p2[:, :TT], 1.0 / 512, msq[:], ALU.mult, ALU.subtract)
            yield
            S.ts(rs[:], rs[:], EPS, None, ALU.add)
            S.recip(rs[:], rs[:])
            yield
            S.act(rs[:], rs[:], AF.Sqrt)
            S.tt(upre[:], upre[:], mean[:].unsqueeze(1).to_broadcast([128, 4, TT]), ALU.subtract)
            yield
            S.tt(upre[:], upre[:], rs[:].unsqueeze(1).to_broadcast([128, 4, TT]), ALU.mult)
            yield
            for c in range(4):
                S.act(ct[:, c, :], upre[:, c, :], AF.Silu, bias=vec[:, 2, c:c + 1], scale=vec[:, 1, c:c + 1])
            for hh in range(2):
                pp = g.psum()
                for j in range(2):
                    S.mmg(pp[:, j * TT:(j + 1) * TT], [(g.ones[:], hsq[:, hh * 2 + j, :])])
                rv = rsh[:, hh * 2:hh * 2 + 2, :].rearrange("p h t -> p (h t)")
                S.ts(rv, pp[:, 0:2 * TT], 1.0 / 128, EPS, ALU.mult, ALU.add)
                S.recip(rv, rv)
                yield
                S.act(rv, rv, AF.Sqrt)
            yield
            S.tt(hf[b][:], hf[b][:], rsh[:], ALU.mult)
            yield
            for h in range(4):
                S.stt(ct[:, 4 + h, :], hf[b][:, h, :], vec[:, 3, h:h + 1], sgo[b][:, h, :], ALU.mult, ALU.mult)
            yield
            for oc in range(8):
                py = g.psum()
                S.mmg(py[:, :TT], [(Wout[:, k, oc * 128:(oc + 1) * 128], ct[:, k, :]) for k in range(8)])
                S.stt(xt[b][:, oc, :], py[:, :TT], g.mod[:, 0, 16 + oc, r:r + 1], xt[b][:, oc, :], ALU.mult, ALU.add)
                if oc % 2:
                    yield
            S.dma(x1v[:, :, a0:a0 + TT], xt[b][:], q='pool')

        for n in range(0, len(tiles), 2):
            run_zip([tile(m) for m in range(n, min(n + 2, len(tiles)))])


def phase_ffn(g, l, segs):
    S = g.S
    W = TT + 2
    with Phase(S) as ph:
        stg = [ph.sb('stg%d' % i, [128, 1024], F32) for i in range(2)]
        Win = ph.sb('Win', [128, 8, 5632], BF16)
        Wo = ph.sb('Wo', [128, 22, 1024], BF16)
        cw = ph.sb('fcw', [128, 2, 22, 3], F32); cb = ph.sb('fcb', [128, 2, 22], F32)
        S.dma(cw[:], g.I['ffn_cw']); S.dma(cb[:], g.I['ffn_cb'])
        xt = [ph.sb('xt%d' % i, [128, 8, W], F32) for i in range(2)]
        tmp = ph.sb('tmp', [128, 8, W], F32)
        sqh = ph.sb('sqh', [128, 8, W], BF16)
        rst = ph.sb('rst', [128, W], F32)
        hT = [ph.sb('hT%d' % i, [128, 8, W], BF16) for i in range(2)]
        act = [ph.sb('act%d' % i, [128, 22, TT], BF16) for i in range(2)]
        ta = [ph.sb('ta%d' % i, [128, TT], F32) for i in range(2)]
        tb = [ph.sb('tb%d' % i, [128, TT], F32) for i in range(2)]
        def load_x(sg, t0, x):
            sv = sg['src'].rearrange("(k p) t -> p k t", p=128)
            lo = max(t0 - 1, sg['v0']); hi = min(t0 + TT + 1, sg['v1'])
            if lo > t0 - 1:
                S.memset(x[:, :, 0:1], 0.0)
            if hi < t0 + TT + 1:
                S.memset(x[:, :, W - 1:W], 0.0)
            S.dma(x[:, :, lo - (t0 - 1):hi - (t0 - 1)], sv[:, :, lo:hi])
        load_x(segs[0], segs[0]['c0'], xt[0])
        wr = {0: [(0, 1024), (2816, 3840)], 8: [(1024, 2048), (3840, 4864)], 16: [(2048, 2816), (4864, 5632)]}
        it = 0
        for sg in segs:
            sv = sg['src'].rearrange("(k p) t -> p k t", p=128)
            dv = sg['dst'].rearrange("(k p) t -> p k t", p=128)
            r = sg['r']
            for t0 in range(sg['c0'], sg['c1'], TT):
                b = it % 2
                x = xt[b]; h = hT[b]; a = act[b]
                lo = max(t0 - 1, sg['v0']); hi = min(t0 + TT + 1, sg['v1'])
                if it > 0:
                    load_x(sg, t0, x)
                modulate(g, x[:], h, W, l, 3, 4, r, sqh, tmp, rst)
                if lo > t0 - 1:
                    S.memset(h[:, :, 0:1], 0.0)
                if hi < t0 + TT + 1:
                    S.memset(h[:, :, W - 1:W], 0.0)
                for ch in range(22):
                    if it == 0 and ch in wr:
                        load_w(g, ph, Win, g.I['ffn_w_in'][l], 1024, 5632, stg, ranges=wr[ch], q='sp', ce=('dve', 'act'))
                    pg = g.psum(); pv = g.psum()
                    S.mmg(pg[:, 0:W], [(Win[:, k, ch * 128:(ch + 1) * 128], h[:, k, 0:W]) for k in range(8)])
                    S.mmg(pv[:, 0:TT], [(Win[:, k, 2816 + ch * 128:2816 + (ch + 1) * 128], h[:, k, 1:TT + 1]) for k in range(8)])
                    t_a = ta[ch % 2]; t_b = tb[ch % 2]
                    S.act(t_a[:], pg[:, 1:TT + 1], AF.Identity, bias=cb[:, l, ch:ch + 1], scale=cw[:, l, ch, 1:2])
                    S.stt(t_a[:], pg[:, 0:TT], cw[:, l, ch, 0:1], t_a[:], ALU.mult, ALU.add)
                    S.stt(t_a[:], pg[:, 2:TT + 2], cw[:, l, ch, 2:3], t_a[:], ALU.mult, ALU.add)
                    S.act(t_b[:], t_a[:], AF.Gelu_apprx_tanh)
                    S.tt(a[:, ch, :], pv[:, 0:TT], t_b[:], ALU.mult)
                if it == 0:
                    load_w(g, ph, Wo, g.I['ffn_w_out'][l], 2816, 1024, stg, q='sp', ce=('dve', 'act'))
                for oc in range(8):
                    py = g.psum()
                    S.mmg(py[:, :TT], [(Wo[:, k, oc * 128:(oc + 1) * 128], a[:, k, :]) for k in range(22)])
                    S.stt(x[:, oc, 1:TT + 1], py[:, :TT], g.mod[:, l, 40 + oc, r:r + 1], x[:, oc, 1:TT + 1], ALU.mult, ALU.add)
                d = sg['d0'] + (t0 - sg['c0'])
                S.dma(dv[:, :, d:d + TT], x[:, :, 1:TT + 1], q='pool')
                it += 1


def phase_l1a(g):
    S = g.S
    T, TC, TA = g.T, g.TC, g.TA
    NU = TT // 128
    with Phase(S) as ph:
        tmpb = [ph.sb('tmp%d' % i, [128, 8, TT], F32) for i in range(2)]
        stg = [tmpb[i][:].rearrange("p k t -> p (k t)")[:, 0:1024] for i in range(2)]
        Win = ph.sb('Win', [128, 8, 800], BF16)
        load_w(g, ph, Win, g.I['mla_w_in'], 1024, 800, stg)
        Wuq = ph.sb('Wuq', [128, 4, 1536], BF16)
        load_w(g, ph, Wuq, g.I['mla_w_uq'], 512, 1536, stg)
        Wkv = ph.sb('Wkv', [128, 2, 2048], BF16)
        load_w(g, ph, Wkv, g.I['mla_w_ukv'], 256, 2048, stg)
        Wk = ph.sb('Wk', [128, 2, 16, 64], BF16)
        Wv = ph.sb('Wv', [128, 2, 16, 64], BF16)
        for k in range(2):
            w4 = Wkv[:, k, :].rearrange("p (h c) -> p h c", h=16)
            S.copy(Wk[:, k, :, :], w4[:, :, 0:64], e='pool')
            S.copy(Wv[:, k, :, :], w4[:, :, 64:128], e='pool')
        qn_ = ph.sb('qn_', [128, 4], F32); kvn = ph.sb('kvn', [128, 2], F32)
        gain = ph.sb('gain', [128, 2, 96], F32)
        S.dma(qn_[:], g.I['mla_qn']); S.dma(kvn[:], g.I['mla_kvn']); S.dma(gain[:], g.I['mla_gain_rep'])
        xt = [ph.sb('xt%d' % i, [128, 8, TT], F32) for i in range(1)] * 2
        rstb = [ph.sb('rst%d' % i, [128, TT], F32) for i in range(2)]
        hT = [ph.sb('hT%d' % i, [128, 8, TT], BF16) for i in range(2)]
        cqfb = [ph.sb('cqf%d' % i, [128, 4, TT], F32) for i in range(2)]
        cqnb = [ph.sb('cqn%d' % i, [128, 4, TT], BF16) for i in range(2)]
        ckfb = [tmpb[i][:, 6:8, :] for i in range(2)]
        cknb = [ph.sb('ckn%d' % i, [128, 2, TT], BF16) for i in range(2)]
        rs2b = [ph.sb('rs2_%d' % i, [128, TT], F32) for i in range(2)]
        cs = [ph.sb('cs%d' % i, [128, NU, 2, 16], F32) for i in range(2)]
        src = [ph.sb('src%d' % i, [128, 16, 96], F32) for i in range(4)]
        sqb = [ph.sb('sqb%d' % i, [128, 16, 96], F32) for i in range(4)]
        ssum = [ph.sb('ssum%d' % i, [128, 16], F32) for i in range(4)]
        krs = [ph.sb('krs%d' % i, [128, 32], F32) for i in range(4)]
        ra = [ph.sb('ra%d' % i, [128, 16, 16], F32) for i in range(4)]
        rb = [ph.sb('rb%d' % i, [128, 16, 16], F32) for i in range(4)]
        rc = [ph.sb('rc%d' % i, [128, 16, 16], F32) for i in range(4)]
        rd = [ph.sb('rd%d' % i, [128, 16, 16], F32) for i in range(4)]
        dstb = [ph.sb('dstb%d' % i, [128, 16, 128], BF16) for i in range(4)]
        for i in range(4):
            S.memset(dstb[i][:], 0.0)
        KTt = [ph.sb('KTt%d' % i, [128, 16, TT], BF16) for i in range(1)] * 2
        QTt = [ph.sb('QTt%d' % i, [128, 16, TT], BF16) for i in range(1)] * 2
        vsb = [ph.sb('vsb%d' % i, [128, 16, 128], BF16) for i in range(2)]
        for i in range(2):
            v4 = vsb[i][:].rearrange("p (hp two) c -> p hp two c", two=2)
            S.memset(v4[:, :, 0, 64:128], 1.0)
            S.memset(v4[:, :, 1, 0:64], 1.0)
        cnt = [0]

        def nr_tok(i, gi, rope, outT, u):
            x_ = src[i]; d_ = dstb[i]
            S.act(sqb[i][:], x_[:], AF.Square)
            yield
            S.op('dve', lambda E: E.reduce_sum(out=ssum[i][:], in_=sqb[i][:], axis=mybir.AxisListType.X), [sqb[i][:]], [ssum[i][:]])
            yield
            S.ts(ssum[i][:], ssum[i][:], 1.0 / 96, EPS, ALU.mult, ALU.add)
            S.recip(ssum[i][:], ssum[i][:])
            yield
            S.act(ssum[i][:], ssum[i][:], AF.Sqrt)
            yield
            S.tt(x_[:], x_[:], ssum[i][:].unsqueeze(2).to_broadcast([128, 16, 96]), ALU.mult)
            yield
            gt_ = gain[:, gi, :].unsqueeze(1).to_broadcast([128, 16, 96])
            if rope is None:
                S.tt(d_[:, :, 0:96], x_[:], gt_, ALU.mult, e='pool')
                yield
            else:
                S.tt(x_[:], x_[:], gt_, ALU.mult, e='pool')
                yield
                S.copy(d_[:, :, 0:64], x_[:, :, 0:64], e='act')
                r4 = x_[:, :, 64:96].rearrange("p h (q two) -> p h q two", two=2)
                o4 = d_[:, :, 64:96].rearrange("p h (q two) -> p h q two", two=2)
                c_ = rope[:, u, 0, :].unsqueeze(1).to_broadcast([128, 16, 16])
                s_ = rope[:, u, 1, :].unsqueeze(1).to_broadcast([128, 16, 16])
                S.tt(ra[i][:], r4[:, :, :, 0], c_, ALU.mult)
                S.tt(rb[i][:], r4[:, :, :, 1], s_, ALU.mult, e='pool')
                S.tt(rc[i][:], r4[:, :, :, 0], s_, ALU.mult)
                S.tt(rd[i][:], r4[:, :, :, 1], c_, ALU.mult, e='pool')
                yield
                S.tt(o4[:, :, :, 0], ra[i][:], rb[i][:], ALU.subtract)
                S.tt(o4[:, :, :, 1], rc[i][:], rd[i][:], ALU.add, e='pool')
                yield
            for hb in range(4):
                pT = g.psum()
                for j in range(4):
                    S.mmg(pT[:, j * 128:(j + 1) * 128], [(d_[:, hb * 4 + j, :], g.identb[:])])
                S.copy(outT[:, hb * 4:hb * 4 + 4, u * 128:(u + 1) * 128],
                       pT[:, 0:512].rearrange("p (j t) -> p j t", j=4), e='act')
                if hb % 2:
                    yield

        def k_chain(i, u, h, rope, kt_, a0, ckn):
            us = slice(u * 128, (u + 1) * 128)
            for half in range(2):
                pk = g.psum()
                S.mmg(pk[:, 0:512], [(ckn[:, k, us], Wk[:, k, half * 8:(half + 1) * 8, :].rearrange("p h c -> p (h c)"))
                                     for k in range(2)])
                S.copy(src[i][:, half * 8:(half + 1) * 8, 0:64], pk[:, 0:512].rearrange("p (h c) -> p h c", h=8),
                       e=('act' if half else 'dve'))
            pkr = g.psum()
            S.mmg(pkr[:, 0:32], [(h[:, k, us], Win[:, k, 768:800]) for k in range(8)])
            S.copy(krs[i][:], pkr[:, 0:32], e='act')
            yield
            S.copy(src[i][:, :, 64:96], krs[i][:].unsqueeze(1).to_broadcast([128, 16, 32]), e='pool')
            yield
            yield from nr_tok(i, 1, rope, kt_, u)
            vs = vsb[u % 2]
            for half in range(2):
                pv = g.psum()
                S.mmg(pv[:, 0:512], [(ckn[:, k, us],
                                      Wv[:, k, half * 8:(half + 1) * 8, :].rearrange("p h c -> p (h c)")) for k in range(2)])
                p4 = pv[:, 0:512].rearrange("p (hp two c) -> p hp two c", two=2, c=64)
                v4 = vs[:, half * 8:(half + 1) * 8, :].rearrange("p (hp two) c -> p hp two c", two=2)
                S.copy(v4[:, :, 0, 0:64], p4[:, :, 0, :], e='act')
                S.copy(v4[:, :, 1, 64:128], p4[:, :, 1, :], e='act')
                yield
            S.dma(g.Vtok[a0 + u * 128:a0 + (u + 1) * 128], vs[:], q='pool')

        def q_chain(i, u, rope, qt_, cqn):
            us = slice(u * 128, (u + 1) * 128)
            for third in range(3):
                pq = g.psum()
                S.mmg(pq[:, 0:512], [(cqn[:, k, us], Wuq[:, k, third * 512:(third + 1) * 512]) for k in range(4)])
                S.copy(src[i][:].rearrange("p h c -> p (h c)")[:, third * 512:(third + 1) * 512], pq[:, 0:512],
                       e=('act' if third % 2 else 'dve'))
            yield
            yield from nr_tok(i, 0, rope, qt_, u)

        x2v = g.x2T.rearrange("(k p) t -> p k t", p=128)
        tiles = []
        for seq in g.seqs:
            for ti in range(seq['T'] // TT):
                tiles.append((seq, ti))

        def prologue(n):
            seq, ti = tiles[n]
            o_, r = seq['o'], seq['r']
            t0 = ti * TT
            a0 = o_ + t0
            b = n % 2
            x = xt[b]; h = hT[b]; tmp = tmpb[b]; rst = rstb[b]
            ckf = ckfb[b]; ckn = cknb[b]; cqf = cqfb[b]; cqn = cqnb[b]; rs2 = rs2b[b]
            cks = tmp[:, 0:2, :]; cqs = tmp[:, 2:6, :]
            S.dma(x[:], x2v[:, :, a0:a0 + TT])
            is_lat = (r == 0)
            do_q = is_lat and (t0 < T // 2 + 2)
            if is_lat:
                S.dma(cs[b][:], g.I['ropeCS'][t0:t0 + TT].rearrange("(u p) a c -> p u a c", p=128))
            yield
            S.act(tmp[:], x[:], AF.Square)
            yield
            pp_ = g.psum()
            S.mmg(pp_[:, :TT], [(g.ones[:], tmp[:, k, :]) for k in range(8)])
            S.ts(rst[:], pp_[:, :TT], 1.0 / D, EPS, ALU.mult, ALU.add)
            S.recip(rst[:], rst[:])
            yield
            S.act(rst[:], rst[:], AF.Sqrt)
            yield
            S.tt(tmp[:], x[:], rst[:].unsqueeze(1).to_broadcast([128, 8, TT]), ALU.mult)
            yield
            for c in range(8):
                S.act(h[:, c, :], tmp[:, c, :], AF.Identity, bias=g.mod[:, 1, c, r:r + 1], scale=g.mod[:, 1, 8 + c, r:r + 1])
            yield
            for c in range(2):
                pc = g.psum()
                S.mmg(pc[:, :TT], [(Win[:, k, 512 + c * 128:512 + (c + 1) * 128], h[:, k, :]) for k in range(8)])
                S.copy(ckf[:, c, :], pc[:, :TT], e='act')
            yield
            S.act(cks[:], ckf[:], AF.Square)
            yield
            pst = g.psum()
            S.mmg(pst[:, :TT], [(g.ones[:], cks[:, c, :]) for c in range(2)])
            S.ts(rs2[:], pst[:, :TT], 1.0 / 256, EPS, ALU.mult, ALU.add)
            S.recip(rs2[:], rs2[:])
            yield
            S.act(rs2[:], rs2[:], AF.Sqrt)
            yield
            S.tt(ckf[:], ckf[:], rs2[:].unsqueeze(1).to_broadcast([128, 2, TT]), ALU.mult)
            for c in range(2):
                S.ts(ckn[:, c, :], ckf[:, c, :], kvn[:, c:c + 1], None, ALU.mult)
            yield
            if do_q:
                for c in range(4):
                    pc = g.psum()
                    S.mmg(pc[:, :TT], [(Win[:, k, c * 128:(c + 1) * 128], h[:, k, :]) for k in range(8)])
                    S.copy(cqf[:, c, :], pc[:, :TT], e='act')
                yield
                S.act(cqs[:], cqf[:], AF.Square)
                yield
                pst = g.psum()
                S.mmg(pst[:, :TT], [(g.ones[:], cqs[:, c, :]) for c in range(4)])
                S.ts(rs2[:], pst[:, :TT], 1.0 / 512, EPS, ALU.mult, ALU.add)
                S.recip(rs2[:], rs2[:])
                yield
                S.act(rs2[:], rs2[:], AF.Sqrt)
                yield
                S.tt(cqf[:], cqf[:], rs2[:].unsqueeze(1).to_broadcast([128, 4, TT]), ALU.mult)
                for c in range(4):
                    S.ts(cqn[:, c, :], cqf[:, c, :], qn_[:, c:c + 1], None, ALU.mult)

        def tile_chains(n):
            seq, ti = tiles[n]
            o_, r = seq['o'], seq['r']
            t0 = ti * TT
            a0 = o_ + t0
            b = n % 2
            is_lat = (r == 0)
            do_q = is_lat and (t0 < T // 2 + 2)
            rope = cs[b] if is_lat else None
            kt_ = KTt[b]; qt_ = QTt[b]
            chains = [k_chain(u, u, hT[b], rope, kt_, a0, cknb[b]) for u in range(NU)]
            if do_q:
                chains += [q_chain(2 + u, u, rope, qt_, cqnb[b]) for u in range(NU)]
            return chains, (a0, t0, do_q, kt_, qt_)

        run_zip([prologue(0)])
        for n in range(len(tiles)):
            chains, (a0, t0, do_q, kt_, qt_) = tile_chains(n)
            if n + 1 < len(tiles):
                chains.append(prologue(n + 1))
            run_zip(chains)
            S.dma(g.KT[:, :, a0:a0 + TT].rearrange("h d t -> d h t"), kt_[0:96], q='pool')
            if do_q:
                S.dma(g.QT[:, :, t0:t0 + TT].rearrange("h d t -> d h t"), qt_[0:96], q='pool')


def phase_l1b(g):
    S = g.S
    T, TC, TA = g.T, g.TC, g.TA
    NKT = TA // 128
    QW = 512
    qtiles = [(q0, QW) for q0 in range(0, T // 2, QW)] + [(T // 2, 2)]
    scale = 96.0 ** -0.5
    with Phase(S) as ph:
        KT = [ph.sb('KT%d' % i, [128, TA], BF16) for i in range(2)]
        V = [ph.sb('V%d' % i, [128, NKT, 128], BF16) for i in range(2)]
        Q = [ph.sb('Q%d' % i, [128, QW], BF16) for i in range(2)]
        for i in range(2):
            S.memset(KT[i][96:128, :], 0.0); S.memset(Q[i][96:128, :], 0.0)
        P = [ph.sb('P%d' % i, [128, 2, QW], BF16) for i in range(3)]
        osb = [ph.sb('osb%d' % i, [128, QW], F32) for i in range(2)]
        oo = [ph.sb('oo%d' % i, [128, QW], BF16) for i in range(2)]
        shA = ph.sb('shA', [128, 128], F32); shB = ph.sb('shB', [128, 128], F32)
        S.memset(shA[:], 0.0); S.memset(shB[:], 0.0)
        S.copy(shA[64:128, 0:64], g.ident[64:128, 64:128], e='pool')
        S.copy(shB[0:64, 64:128], g.ident[0:64, 0:64], e='pool')
        qi = 0
        pi = 0
        si = 0
        for h in range(16):
            kt_sb = KT[h % 2]; v_sb = V[h % 2]
            S.dma(kt_sb[0:96, :], g.KT[h])
            S.dma(v_sb[:], g.Vtok[:, h, :].rearrange("(n p) c -> p n c", p=128))
            odd = h % 2
            for (q0, w) in qtiles:
                q = Q[qi % 2]; ob = osb[qi % 2]; o2 = oo[qi % 2]
                qi += 1
                S.dma(q[0:96, 0:w], g.QT[h, :, q0:q0 + w])
                pO = g.ps[0]
                LA = 2
                psl = {}
                NP = NKT // 2

                def issue_s(kp):
                    nonlocal si
                    pdb = g.pd[1 + si % 3]
                    si += 1
                    for j in range(2):
                        kt = 2 * kp + j
                        S.mm(pdb[:, j * 512:j * 512 + w], kt_sb[:, kt * 128:(kt + 1) * 128], q[:, 0:w])
                    psl[kp] = pdb
                for kp in range(min(LA, NP)):
                    issue_s(kp)
                for kp in range(NP):
                    if kp + LA < NP:
                        issue_s(kp + LA)
                    pdb = psl.pop(kp)
                    p_ = P[pi % 3]
                    pi += 1
                    S.act(p_[:, :, 0:w], pdb[:, :].rearrange("p (b c) -> p b c", b=2)[:, :, 0:w], AF.Exp, scale=scale)
                    for j in range(2):
                        kt = 2 * kp + j
                        S.op('pe', lambda E, p_=p_, kt=kt, j=j: E.matmul(pO[:, 0:w], lhsT=v_sb[:, kt, :], rhs=p_[:, j, 0:w],
                                                                         start=(kt == 0), stop=(kt == NKT - 1)),
                             [v_sb[:, kt, :], p_[:, j, 0:w]], [pO[:, 0:w]], inc=True)
                S.copy(ob[:, 0:w], pO[:, 0:w], e='act')
                dr = slice(0, 64) if odd else slice(64, 128)
                nr_ = slice(64, 128) if odd else slice(0, 64)
                S.recip(ob[dr, 0:w], ob[dr, 0:w])
                pb = g.ps[1]
                S.mmg(pb[:, 0:w], [((shB if odd else shA)[:], ob[:, 0:w])])
                S.tt(o2[nr_, 0:w], ob[nr_, 0:w], pb[nr_, 0:w], ALU.mult)
                S.dma(g.OT[h * 64:(h + 1) * 64, q0:q0 + w], o2[nr_, 0:w], q='pool')


def phase_l1c(g):
    S = g.S
    T, TC = g.T, g.TC
    NQ = T // 2 + 2
    with Phase(S) as ph:
        stg = [ph.sb('stg%d' % i, [128, 2048], F32) for i in range(2)]
        Wmo = ph.sb('Wmo', [128, 8, 1024], BF16)
        load_w(g, ph, Wmo, g.I['mla_w_out'], 1024, 1024, stg)
        ot = [ph.sb('ot%d' % i, [128, 8, TT], BF16) for i in range(2)]
        xt = [ph.sb('xt%d' % i, [128, 8, TT], F32) for i in range(2)]
        otv = g.OT.rearrange("(k p) t -> p k t", p=128)
        x2v = g.x2T.rearrange("(k p) t -> p k t", p=128)
        x3v = g.x3T.rearrange("(k p) t -> p k t", p=128)
        it = 0
        for t0 in range(0, NQ, TT):
            w = min(TT, NQ - t0)
            b = it % 2
            it += 1
            S.dma(ot[b][:, :, 0:w], otv[:, :, t0:t0 + w])
            S.dma(xt[b][:, :, 0:w], x2v[:, :, TC + t0:TC + t0 + w])
            for oc in range(8):
                py = g.psum()
                S.mmg(py[:, :w], [(Wmo[:, k, oc * 128:(oc + 1) * 128], ot[b][:, k, 0:w]) for k in range(8)])
                S.stt(xt[b][:, oc, 0:w], py[:, :w], g.mod[:, 1, 16 + oc, 0:1], xt[b][:, oc, 0:w], ALU.mult, ALU.add)
            S.dma(x3v[:, :, t0:t0 + w], xt[b][:, :, 0:w], q='pool')


def _consts():
    f = np.float32
    m = {}
    m['c_ident'] = np.eye(128, dtype=f)
    m['c_ones'] = np.ones((128, 128), f)
    m['c_triu'] = np.triu(np.ones((128, 128), f))
    m['c_tril'] = np.tril(np.ones((128, 128), f))
    return m


def _rope(T):
    f = np.float32
    rows = T // 64
    row = np.repeat(np.arange(rows, dtype=f), 64)
    col = np.tile(np.arange(64, dtype=f), rows)
    half = 16
    inv = (f(10000.0) ** (-np.arange(0, half, 2, dtype=f) / f(half))).astype(f)
    ang = np.concatenate([row[:, None] * inv, col[:, None] * inv], axis=-1).astype(f)
    c = np.cos(ang).astype(f); s = np.sin(ang).astype(f)
    return np.stack([c, s], axis=1)


def _pp(v, nch):
    return np.ascontiguousarray(np.asarray(v, np.float32).reshape(nch, 128).T)


def prep(inp, b, s, T):
    f = np.float32
    rev = (s == 1)
    A = lambda a: np.ascontiguousarray(np.asarray(a, f))
    x = np.asarray(inp['x'][b], f); ctx = np.asarray(inp['ctx'][b], f)
    if rev:
        x = x[::-1]; ctx = ctx[::-1]
    m = dict(_consts())
    m['xT'] = A(x.T); m['ctxT'] = A(ctx.T)
    cc = np.stack([np.asarray(inp['c'][b], f), np.asarray(inp['c_ctx'], f)], axis=-1)
    m['cc'] = A(cc.reshape(8, 128, 2).transpose(1, 0, 2))
    m['ada_w'] = A(inp['ada_w'])
    m['ada_b'] = A(np.asarray(inp['ada_b'], f).reshape(2, 48, 128).transpose(2, 0, 1))
    w = np.array(inp['ab_w_in'][0], f)
    gbias = np.array(inp['ab_gate_bias'][0], f)
    cw = np.asarray(inp['ab_conv_w'][0], f)
    if rev:
        w = np.concatenate([w[:, :2560], w[:, 2568:2576], w[:, 2560:2568]], axis=1)
        gbias = np.concatenate([gbias[8:16], gbias[0:8]])
        cw = cw[::-1]
    m['ab_w_in'] = A(w)
    m['gate_bias'] = A(np.broadcast_to(gbias, (128, 16)))
    m['conv_w'] = A(cw.T.reshape(4, 128, 31).transpose(1, 0, 2))
    vec = np.stack([np.asarray(inp[k][0], f) for k in ('ab_conv_b', 'ab_ln_g', 'ab_ln_b', 'ab_head_gain')])
    m['ab_vec'] = A(vec.reshape(4, 4, 128).transpose(2, 0, 1))
    m['ab_w_out'] = A(inp['ab_w_out'][0])
    m['ffn_w_in'] = A(inp['ffn_w_in'])
    fcw = np.asarray(inp['ffn_conv_w'], f)
    if rev:
        fcw = fcw[:, ::-1]
    m['ffn_cw'] = A(fcw.reshape(2, 3, 22, 128).transpose(3, 0, 2, 1))
    m['ffn_cb'] = A(np.asarray(inp['ffn_conv_b'], f).reshape(2, 22, 128).transpose(2, 0, 1))
    m['ffn_w_out'] = A(inp['ffn_w_out'])
    m['mla_w_in'] = A(inp['mla_w_in'][0])
    m['mla_qn'] = _pp(inp['mla_q_norm'][0], 4)
    m['mla_kvn'] = _pp(inp['mla_kv_norm'][0], 2)
    m['mla_w_uq'] = A(inp['mla_w_uq'][0]); m['mla_w_ukv'] = A(inp['mla_w_ukv'][0])
    gn = np.stack([np.asarray(inp['mla_q_gain'][0], f), np.asarray(inp['mla_k_gain'][0], f)], axis=0)
    m['mla_gain_rep'] = A(np.broadcast_to(gn, (128, 2, 96)))
    m['mla_w_out'] = A(inp['mla_w_out'][0])
    cs = _rope(T)
    if rev:
        cs = cs[::-1]
    m['ropeCS'] = A(cs)
    return m


_NC_CACHE = {}


def kernel(**inputs):
    T = inputs['x'].shape[1]
    B = inputs['x'].shape[0]
    if T not in _NC_CACHE:
        _NC_CACHE[T] = build(T)
    nc = _NC_CACHE[T]
    maps = []
    for b in range(B):
        for s in range(2):
            maps.append(prep(inputs, b, s, T))
    res = run_bass_kernel_spmd(nc, maps, core_ids=list(range(2 * B)))
    out = np.empty((B, T, D), np.float32)
    for b in range(B):
        for s in range(2):
            o = res.results[2 * b + s]['outT'].T
            if s == 0:
                out[b, :T // 2] = o
            else:
                out[b, T // 2:] = o[::-1]
    return out
```
